# Optimizing a Trainium2 kernel written in Bass

```python
import math
import jax, jax.numpy as jnp
from jax import lax
import numpy as np

D_MODEL = 1024
BATCH = 4
SEQ = 4096
DEPTH = 4
DEC_BATCH = 128
DEC_SEQ = 1
PAST_LEN = 2048
PAGE_SIZE = 128

D_MIX = D_MODEL
D_BR = D_MIX // 4
HD = 64
H_A = D_BR // HD
H_B = D_BR // HD
H_C = D_BR // HD
CHUNK_A = 64
LB_FLOOR = 1e-30
LORA_W = 32
LORA_A = 32
RWKV_GN_EPS = 64e-5
C_CONFIGS = ((128, 1), (512, 4), (2048, 16))
C_WIN_MAX = 2048
Q_BLOCK = 128
MASK_VALUE = -1e30
ROPE_THETA = 10000.0
POOL_WINDOWS = (2, 4, 8, 16)
POOL_MAX = 16
D_POOL_G = D_BR // len(POOL_WINDOWS)
EPS = 1e-6
A_W = 4 * D_BR
B_SHIFT_W = 3 * D_BR + LORA_W + LORA_A
B_W = B_SHIFT_W + D_BR
C_W = 4 * D_BR
D_W = 2 * D_BR
D_IN = A_W + B_W + C_W + D_W

kernel_name = "hymba_hgrn2_rwkv7_dilated_pool_decoder_step"

F32 = jnp.float32


def rms_norm(x, g, eps=EPS):
    xf = x.astype(F32)
    y = xf * lax.rsqrt(jnp.mean(xf * xf, axis=-1, keepdims=True) + eps)
    return (y * g.astype(F32)).astype(x.dtype)


def rope(x, pos):
    half = x.shape[-1] // 2
    inv_freq = jnp.float32(ROPE_THETA) ** (-jnp.arange(half, dtype=F32) / half)
    ang = pos.astype(F32)[:, None] * inv_freq[None, :]
    cos = jnp.cos(ang)[None, :, None, :]
    sin = jnp.sin(ang)[None, :, None, :]
    xf = x.astype(F32)
    x1, x2 = xf[..., :half], xf[..., half:]
    return jnp.concatenate([x1 * cos - x2 * sin, x2 * cos + x1 * sin], axis=-1).astype(x.dtype)


def hgrn2_scan(q, logf, k, v, s0):
    B, T, H, DK = q.shape
    C = math.gcd(T, CHUNK_A)
    n = T // C

    def to_chunks(a):
        return a.reshape(B, n, C, H, a.shape[-1]).transpose(1, 0, 3, 2, 4)

    xs = tuple(to_chunks(a) for a in (q, logf, k, v))
    causal = jnp.tril(jnp.ones((C, C), dtype=bool))[:, :, None]

    def step(s, inp):
        qb, gb, kb, vb = inp
        G = jnp.cumsum(gb, axis=2)
        o_inter = jnp.einsum('bhtk,bhkv->bhtv', qb * jnp.exp(G), s)
        diff = G[:, :, :, None, :] - G[:, :, None, :, :]
        decay = jnp.where(causal, jnp.exp(jnp.where(causal, diff, 0.0)), 0.0)
        att = jnp.einsum('bhtk,bhsk,bhtsk->bhts', qb, kb, decay)
        o_intra = jnp.einsum('bhts,bhsv->bhtv', att, vb)
        G_last = G[:, :, -1:, :]
        s_new = jnp.exp(G_last[:, :, 0, :])[..., None] * s + jnp.einsum(
            'bhsk,bhsv->bhkv', kb * jnp.exp(G_last - G), vb)
        return s_new, o_inter + o_intra

    s_fin, o = lax.scan(step, s0, xs)
    o = o.transpose(1, 0, 3, 2, 4).reshape(B, T, H, v.shape[-1])
    return o, s_fin


def rwkv7_scan(r, logw, k, v, a_vec, b_vec, s0):
    def step(s, inp):
        rt, wt, kt, vt, at, bt = inp
        sa = jnp.einsum('bhij,bhj->bhi', s, at)
        s = (s * jnp.exp(wt)[:, :, None, :] + sa[..., None] * bt[:, :, None, :]
             + vt[..., None] * kt[:, :, None, :])
        return s, jnp.einsum('bhij,bhj->bhi', s, rt)

    xs = tuple(jnp.moveaxis(a, 1, 0) for a in (r, logw, k, v, a_vec, b_vec))
    s_fin, y = lax.scan(step, s0, xs)
    return jnp.moveaxis(y, 0, 1), s_fin


def dilated_attention(q, k_ext, v_ext):
    B, T, H, Dh = q.shape
    P = k_ext.shape[1] - T
    QB = math.gcd(T, Q_BLOCK)
    nb = T // QB
    scale = Dh ** -0.5
    q_blocks = q.reshape(B, nb, QB, H, Dh).transpose(1, 0, 2, 3, 4)

    def block(args):
        qblk, bi = args
        e = P + bi * QB + jnp.arange(QB)
        outs, lses = [], []
        for (w, d) in C_CONFIGS:
            j = jnp.arange(w // d + 1)
            idx = e[:, None] - j[None, :] * d
            valid = idx >= 0
            idxc = jnp.maximum(idx, 0)
            kg = k_ext[:, idxc]
            vg = v_ext[:, idxc]
            s = jnp.einsum('bqhd,bqnhd->bhqn', qblk, kg, preferred_element_type=F32) * scale
            s = jnp.where(valid[None, None], s, MASK_VALUE)
            m = jnp.max(s, axis=-1, keepdims=True)
            p = jnp.exp(s - m)
            den = jnp.sum(p, axis=-1)
            o = jnp.einsum('bhqn,bqnhd->bhqd', p, vg.astype(F32)) / den[..., None]
            outs.append(o)
            lses.append(m[..., 0] + jnp.log(den))
        wts = jax.nn.softmax(jnp.stack(lses, axis=0), axis=0)
        o = jnp.sum(wts[..., None] * jnp.stack(outs, axis=0), axis=0)
        return o.transpose(0, 2, 1, 3)

    out = lax.map(block, (q_blocks, jnp.arange(nb)))
    return out.transpose(1, 0, 2, 3, 4).reshape(B, T, H, Dh).astype(q.dtype)


def multiscale_pool(u_ext, pos0, T):
    P = u_ext.shape[1] - T
    cs = jnp.cumsum(jnp.pad(u_ext.astype(F32), ((0, 0), (POOL_MAX, 0), (0, 0))), axis=1)
    e = P + jnp.arange(T)
    pos = pos0 + e
    hi = cs[:, POOL_MAX + e]
    outs = []
    for g, w in enumerate(POOL_WINDOWS):
        sl = slice(g * D_POOL_G, (g + 1) * D_POOL_G)
        s = hi[..., sl] - cs[:, POOL_MAX + e - w, sl]
        cnt = jnp.minimum(w, pos + 1).astype(F32)
        outs.append(s / cnt[None, :, None])
    return jnp.concatenate(outs, axis=-1) - u_ext[:, P:].astype(F32)


def mixer_hgrn2(pa, s0, lb, onorm_g):
    B, T, _ = pa.shape
    q, fl, i, g = jnp.split(pa.astype(F32), 4, axis=-1)
    q = jax.nn.silu(q)
    logf = jnp.logaddexp(jax.nn.log_sigmoid(fl),
                         jnp.log(jnp.maximum(lb, LB_FLOOR)) + jax.nn.log_sigmoid(-fl))
    k = jnp.exp(jnp.log1p(-lb) + jax.nn.log_sigmoid(-fl))
    heads = lambda a: a.reshape(B, T, H_A, HD)
    o, s_fin = hgrn2_scan(heads(q), heads(logf), heads(k), heads(i), s0.astype(F32))
    o = rms_norm(o, onorm_g.reshape(H_A, HD)).reshape(B, T, D_BR)
    return o * jax.nn.silu(g), s_fin


def mixer_rwkv7(pb, shift0, s0, mu, w0, w2, a0, a2, k_k, k_a, r_k, gn_w, gn_b):
    B, T, _ = pb.shape
    pbf = pb.astype(F32)
    xs, g = pbf[..., :B_SHIFT_W], pbf[..., B_SHIFT_W:]
    prev = jnp.concatenate([shift0[:, None, :].astype(F32), xs[:, :-1]], axis=1)
    xm = xs + (prev - xs) * mu
    r, k, v, xw, xa = jnp.split(xm, [D_BR, 2 * D_BR, 3 * D_BR, 3 * D_BR + LORA_W], axis=-1)
    w = -jax.nn.softplus(-(w0 + jnp.tanh(xw) @ w2)) - 0.5
    logw = -jnp.exp(w)
    a = jax.nn.sigmoid(a0 + xa @ a2)
    heads = lambda t: t.reshape(B, T, H_B, HD)
    kk = heads(k * k_k)
    kk = kk / jnp.maximum(jnp.sqrt(jnp.sum(kk * kk, axis=-1, keepdims=True)), 1e-12)
    k = k * (1.0 + (a - 1.0) * k_a)
    rh, kh, vh, ah = heads(r), heads(k), heads(v), heads(a)
    y, s_fin = rwkv7_scan(rh, heads(logw), kh, vh, -kk, kk * ah, s0.astype(F32))
    mean = jnp.mean(y, axis=-1, keepdims=True)
    var = jnp.mean(jnp.square(y - mean), axis=-1, keepdims=True)
    y = ((y - mean) * lax.rsqrt(var + RWKV_GN_EPS)).reshape(B, T, D_BR) * gn_w + gn_b
    bonus = jnp.sum(rh * kh * r_k.reshape(H_B, HD), axis=-1, keepdims=True) * vh
    out = (y + bonus.reshape(B, T, D_BR)) * jax.nn.silu(g)
    return out, s_fin, xs[:, -1]


def mixer_dilated(pc, kbuf, vbuf, pos, qn_g, kn_g):
    B, T, _ = pc.shape
    q, k, v, g = jnp.split(pc, 4, axis=-1)
    heads = lambda t: t.reshape(B, T, H_C, HD)
    q = rope(rms_norm(heads(q), qn_g), pos)
    k = rope(rms_norm(heads(k), kn_g), pos)
    v = heads(v)
    k_ext = jnp.concatenate([kbuf.astype(k.dtype), k], axis=1)
    v_ext = jnp.concatenate([vbuf.astype(v.dtype), v], axis=1)
    o = dilated_attention(q, k_ext, v_ext).reshape(B, T, D_BR)
    return o.astype(F32) * jax.nn.silu(g.astype(F32)), k, v


def mixer_pool(pd, pbuf, t0, w_pool, p_scale):
    B, T, _ = pd.shape
    u, g = jnp.split(pd, 2, axis=-1)
    u_ext = jnp.concatenate([pbuf.astype(u.dtype), u], axis=1)
    pooled = multiscale_pool(u_ext, t0 - pbuf.shape[1], T)
    mixed = jnp.einsum('btgc,gcd->btgd', pooled.reshape(B, T, len(POOL_WINDOWS), D_POOL_G),
                       w_pool.astype(F32)).reshape(B, T, D_BR) * p_scale
    return mixed * jax.nn.silu(g.astype(F32)), u


def run_group(x, c, t0, sA, sBw, sBs, ck, cv, cd,
              ada_w, ada_b, norm_g, w_in, w_out, a_lb_logits, a_onorm_g,
              b_mu, b_w0, b_w2, b_a0, b_a2, b_k_k, b_k_a, b_r_k, b_gn_w, b_gn_b,
              c_qnorm_g, c_knorm_g, d_w_pool, d_scale):
    B, T, _ = x.shape
    pos = t0 + jnp.arange(T)
    lb_sm = jax.nn.softmax(a_lb_logits.astype(F32), axis=0)
    lower_bounds = jnp.cumsum(lb_sm, axis=0) - lb_sm[0:1]
    c_keep = min(C_WIN_MAX, T)
    d_keep = min(POOL_MAX - 1, T)
    nA, nBw, nBs, nK, nV, nD = [], [], [], [], [], []
    for l in range(DEPTH):
        mod = jax.nn.silu(c) @ ada_w[l] + ada_b[l]
        shift, scale, gate = jnp.split(mod, 3, axis=-1)
        h = rms_norm(x, norm_g[l]) * (1.0 + scale[:, None, :]) + shift[:, None, :]
        proj = h @ w_in[l]
        pa, pb, pc, pd = jnp.split(proj, [A_W, A_W + B_W, A_W + B_W + C_W], axis=-1)
        oA, sA_new = mixer_hgrn2(pa, sA[l], lower_bounds[l], a_onorm_g[l])
        oB, sBw_new, sBs_new = mixer_rwkv7(pb, sBs[l], sBw[l], b_mu[l], b_w0[l], b_w2[l], b_a0[l],
                                           b_a2[l], b_k_k[l], b_k_a[l], b_r_k[l], b_gn_w[l], b_gn_b[l])
        oC, k_new, v_new = mixer_dilated(pc, ck[l], cv[l], pos, c_qnorm_g[l], c_knorm_g[l])
        oD, u_new = mixer_pool(pd, cd[l], t0, d_w_pool[l], d_scale[l])
        mix = jnp.concatenate([oA, oB, oC, oD], axis=-1).astype(x.dtype)
        x = x + gate[:, None, :] * (mix @ w_out[l])
        nA.append(sA_new.astype(sA.dtype))
        nBw.append(sBw_new.astype(sBw.dtype))
        nBs.append(sBs_new.astype(sBs.dtype))
        nK.append(k_new[:, T - c_keep:].astype(ck.dtype))
        nV.append(v_new[:, T - c_keep:].astype(cv.dtype))
        nD.append(u_new[:, T - d_keep:].astype(cd.dtype))
    return (x, jnp.stack(nA), jnp.stack(nBw), jnp.stack(nBs),
            jnp.stack(nK), jnp.stack(nV), jnp.stack(nD))


def setup_inputs(seed: int = 0) -> dict:
    key = jax.random.key(seed)
    ks = jax.random.split(key, 40)
    nrm = lambda k, s, sc=1.0: sc * jax.random.normal(k, s, dtype=F32)
    c_buf = min(C_WIN_MAX, PAST_LEN)
    return {
        "x_prompt": nrm(ks[0], (BATCH, SEQ, D_MODEL)),
        "x_sample": nrm(ks[1], (DEC_BATCH, DEC_SEQ, D_MODEL)),
        "state_A": nrm(ks[2], (DEPTH, DEC_BATCH, H_A, HD, HD), 0.3),
        "state_B_wkv": nrm(ks[3], (DEPTH, DEC_BATCH, H_B, HD, HD), 0.3),
        "state_B_shift": nrm(ks[4], (DEPTH, DEC_BATCH, B_SHIFT_W)),
        "cache_C_k": nrm(ks[5], (DEPTH, DEC_BATCH, c_buf, H_C, HD)),
        "cache_C_v": nrm(ks[6], (DEPTH, DEC_BATCH, c_buf, H_C, HD)),
        "cache_D_pool": nrm(ks[7], (DEPTH, DEC_BATCH, POOL_MAX - 1, D_BR)),
        "c_prompt": nrm(ks[8], (BATCH, D_MODEL)),
        "c_sample": nrm(ks[9], (DEC_BATCH, D_MODEL)),
        "ada_w": nrm(ks[10], (DEPTH, D_MODEL, 3 * D_MODEL), D_MODEL ** -0.5),
        "ada_b": nrm(ks[11], (DEPTH, 3 * D_MODEL), 0.02),
        "norm_g": 1.0 + nrm(ks[12], (DEPTH, D_MODEL), 0.02),
        "w_in": nrm(ks[13], (DEPTH, D_MODEL, D_IN), D_MODEL ** -0.5),
        "w_out": nrm(ks[14], (DEPTH, D_MIX, D_MODEL), D_MIX ** -0.5),
        "a_lb_logits": nrm(ks[15], (DEPTH, D_BR), 0.5),
        "a_onorm_g": 1.0 + nrm(ks[16], (DEPTH, D_BR), 0.02),
        "b_mu": jax.random.uniform(ks[17], (DEPTH, B_SHIFT_W), dtype=F32),
        "b_w0": jax.random.uniform(ks[18], (DEPTH, D_BR), dtype=F32, minval=-4.0, maxval=1.0),
        "b_w2": nrm(ks[19], (DEPTH, LORA_W, D_BR), 0.1),
        "b_a0": nrm(ks[20], (DEPTH, D_BR), 0.1),
        "b_a2": nrm(ks[21], (DEPTH, LORA_A, D_BR), 0.1),
        "b_k_k": 0.85 + nrm(ks[22], (DEPTH, D_BR), 0.05),
        "b_k_a": 1.0 + nrm(ks[23], (DEPTH, D_BR), 0.05),
        "b_r_k": nrm(ks[24], (DEPTH, D_BR), 0.1),
        "b_gn_w": 1.0 + nrm(ks[25], (DEPTH, D_BR), 0.02),
        "b_gn_b": nrm(ks[26], (DEPTH, D_BR), 0.02),
        "c_qnorm_g": 1.0 + nrm(ks[27], (DEPTH, HD), 0.02),
        "c_knorm_g": 1.0 + nrm(ks[28], (DEPTH, HD), 0.02),
        "d_w_pool": nrm(ks[29], (DEPTH, len(POOL_WINDOWS), D_POOL_G, D_POOL_G), D_POOL_G ** -0.5),
        "d_scale": 1.0 + nrm(ks[30], (DEPTH, D_BR), 0.1),
    }


def reference(x_prompt, x_sample, state_A, state_B_wkv, state_B_shift, cache_C_k, cache_C_v,
              cache_D_pool, c_prompt, c_sample, ada_w, ada_b, norm_g, w_in, w_out,
              a_lb_logits, a_onorm_g, b_mu, b_w0, b_w2, b_a0, b_a2, b_k_k, b_k_a, b_r_k,
              b_gn_w, b_gn_b, c_qnorm_g, c_knorm_g, d_w_pool, d_scale):
    weights = (ada_w, ada_b, norm_g, w_in, w_out, a_lb_logits, a_onorm_g,
               b_mu, b_w0, b_w2, b_a0, b_a2, b_k_k, b_k_a, b_r_k, b_gn_w, b_gn_b,
               c_qnorm_g, c_knorm_g, d_w_pool, d_scale)
    B = x_prompt.shape[0]
    p_sA = jnp.zeros((DEPTH, B, H_A, HD, HD), state_A.dtype)
    p_sBw = jnp.zeros((DEPTH, B, H_B, HD, HD), state_B_wkv.dtype)
    p_sBs = jnp.zeros((DEPTH, B, B_SHIFT_W), state_B_shift.dtype)
    p_ck = jnp.zeros((DEPTH, B, 0, H_C, HD), cache_C_k.dtype)
    p_cv = jnp.zeros((DEPTH, B, 0, H_C, HD), cache_C_v.dtype)
    p_cd = jnp.zeros((DEPTH, B, 0, D_BR), cache_D_pool.dtype)
    y_prompt, pA, pBw, pBs, pK, pV, pD = run_group(
        x_prompt, c_prompt, 0, p_sA, p_sBw, p_sBs, p_ck, p_cv, p_cd, *weights)
    y_sample, sA, sBw, sBs, sK, sV, sD = run_group(
        x_sample, c_sample, PAST_LEN, state_A, state_B_wkv, state_B_shift,
        cache_C_k, cache_C_v, cache_D_pool, *weights)
    return (y_prompt, y_sample, pA, sA, pBw, sBw, pBs, sBs, pK, sK, pV, sV, pD, sD)
```

```python
import contextlib
import math
import os
import numpy as np
import concourse.bass as bass
import concourse.mybir as mybir
from concourse.bass_utils import run_bass_kernel_spmd

F32 = mybir.dt.float32
BF16 = mybir.dt.bfloat16
U32 = mybir.dt.uint32
AF = mybir.ActivationFunctionType
ALU = mybir.AluOpType
AX = mybir.AxisListType

EPOCH = 30000
D = 1024
DIN = 3648
HD = 64
EPS = 1e-6
GN_EPS = 64e-5
NFM = 1344
NTM = 2304
BSW = 832


class Prog:
    ENGS = ("pe", "act", "dve", "pool", "sp")

    def __init__(self, nc):
        self.nc = nc
        self.ops = {e: [] for e in self.ENGS}
        self.lastw = {}
        self.readers = {}
        self.dma_cnt = {}
        self.dma_keys = []
        self.ecount = {e: 0 for e in self.ENGS}

    def _deps(self, eng, r, w):
        ev = []
        for k in r:
            if k in self.lastw:
                ev.append(('raw', self.lastw[k]))
            if k.startswith("ps"):
                for e in self.readers.get(k, ()):
                    ev.append(('rar', e))
        for k in w:
            if k in self.lastw:
                ev.append(('waw', self.lastw[k]))
            for e in self.readers.get(k, ()):
                ev.append(('war', e))
        out = []
        for kind, e in ev:
            if e[0] == 'E' and e[1] == eng:
                if eng == 'pe' or kind == 'rar':
                    continue
            if e[0] == 'D':
                e = ('D', e[1], self.dma_cnt[e[1]])
            out.append(e)
        return out

    def _commit(self, me, r, w):
        for k in r:
            self.readers.setdefault(k, []).append(me)
        for k in w:
            self.lastw[k] = me
            self.readers[k] = []

    @staticmethod
    def _exp(keys):
        out = []
        for k in keys:
            if k in ("big0", "big1"):
                out += [k + "a", k + "b"]
            else:
                out.append(k)
        return tuple(out)

    def op(self, eng, fn, r=(), w=()):
        r = self._exp(r); w = self._exp(w)
        waits = self._deps(eng, r, w)
        idx = self.ecount[eng]
        self.ecount[eng] += 1
        self.ops[eng].append((fn, waits, ('E', eng, idx)))
        self._commit(('E', eng, idx), r, w)

    def dma(self, fn, key, r=(), w=(), eng="sp"):
        r = self._exp(r); w = self._exp(w)
        waits = self._deps(eng, r, w)
        if key not in self.dma_cnt:
            self.dma_cnt[key] = 0
            self.dma_keys.append(key)
        self.dma_cnt[key] += 1
        me = ('D', key, self.dma_cnt[key])
        self.ops[eng].append((fn, waits, me))
        self._commit(me, r, w)

    def final_wait_all(self, eng="sp"):
        waits = [('D', k, c) for k, c in self.dma_cnt.items()]
        for e2 in self.ENGS:
            if e2 != eng and self.ecount[e2] > 0:
                waits.append(('E', e2, self.ecount[e2] - 1))
        self.ops[eng].append((None, waits, None))

    def emit(self):
        nc = self.nc
        with contextlib.ExitStack() as st:
            esem = {}
            for e in self.ENGS:
                n_ep = (self.ecount[e] + EPOCH - 1) // EPOCH
                for ep in range(max(n_ep, 1)):
                    esem[(e, ep)] = st.enter_context(nc.semaphore(f"s_{e}_{ep}"))
            dsem = {k: st.enter_context(nc.semaphore(f"d_{i}")) for i, k in enumerate(self.dma_keys)}
            block = st.enter_context(nc.Block())
            engobj = {"pe": "tensor", "act": "scalar", "dve": "vector", "pool": "gpsimd", "sp": "sync"}

            def make(e):
                def body(engine):
                    seen = {}
                    for fn, waits, sig in self.ops[e]:
                        for wv in waits:
                            if wv[0] == 'E':
                                ep, c = divmod(wv[2], EPOCH)
                                sem, val = esem[(wv[1], ep)], c + 1
                            else:
                                sem, val = dsem[wv[1]], 16 * wv[2]
                            sid = id(sem)
                            if seen.get(sid, 0) >= val:
                                continue
                            seen[sid] = val
                            engine.wait_ge(sem, val)
                        if fn is None:
                            continue
                        ins = fn(engine)
                        if sig[0] == 'E':
                            ep, _ = divmod(sig[2], EPOCH)
                            ins.then_inc(esem[(e, ep)], 1)
                        else:
                            ins.then_inc(dsem[sig[1]], 16)
                return body

            for e in self.ENGS:
                if self.ops[e]:
                    getattr(block, engobj[e])(make(e))


_FM_COLS = list(range(0, 512)) + list(range(1024, 1856))
_TM_COLS = (list(range(512, 768)) + list(range(3136, 3392)) + list(range(2112, 2624)) +
            list(range(2624, 2880)) + list(range(768, 1024)) + list(range(1856, 2112)) +
            list(range(2880, 3136)) + list(range(3392, 3648)))
_COLS = np.array(_FM_COLS + _TM_COLS)
POOL_W = (2, 4, 8, 16)


def _mult(delta):
    m = ((delta >= 0) & (delta <= 128)).astype(np.float32)
    m += ((delta >= 0) & (delta % 4 == 0) & (delta <= 512))
    m += ((delta >= 0) & (delta % 16 == 0) & (delta <= 2048))
    return m


def host_consts(T):
    c = {}
    c["ident"] = np.eye(128, dtype=np.float32)
    s = np.arange(128)[:, None]
    t = np.arange(128)[None, :]
    c["m_le"] = (s <= t).astype(np.float32)
    c["m_lt"] = (s < t).astype(np.float32)
    c["m_gt"] = (s > t).astype(np.float32)
    am = np.zeros((128, 17, 128), np.float32)
    for r in range(17):
        dl = 16 - r
        am[:, r, :] = _mult(dl * 128 + t - s)
    import ml_dtypes
    c["amask"] = am.astype(ml_dtypes.bfloat16)
    c["identb"] = np.eye(128, dtype=np.float32).astype(ml_dtypes.bfloat16)
    bc = np.zeros((4, 128, 128), np.float32)
    bc0 = np.zeros((4, 128, 128), np.float32)
    bp = np.zeros((4, 128, 128), np.float32)
    for g, w in enumerate(POOL_W):
        inwin = (s <= t) & (s >= t - w + 1)
        bc[g] = inwin / w - (s == t)
        cnt = np.minimum(w, t + 1)
        bc0[g] = inwin / cnt - (s == t)
        bp[g] = ((s - 128) >= (t - w + 1)) / w
    c["band"] = np.concatenate([bc0.transpose(1, 0, 2), bc.transpose(1, 0, 2), bp.transpose(1, 0, 2)], 1)
    c["band"] = np.ascontiguousarray(c["band"], dtype=np.float32)
    half = 32
    inv = (np.float32(10000.0) ** (-np.arange(half, dtype=np.float32) / half)).astype(np.float32)
    pos = np.arange(T, dtype=np.float32)
    ang = (pos[:, None] * inv[None, :]).astype(np.float32)
    c["cos"] = np.ascontiguousarray(np.cos(ang).astype(np.float32).reshape(T // 128, 128, half))
    c["sin"] = np.ascontiguousarray(np.sin(ang).astype(np.float32).reshape(T // 128, 128, half))
    angs = (np.float32(2048.0) * inv).astype(np.float32)
    c["cos_s"] = np.cos(angs).astype(np.float32)[None, :]
    c["sin_s"] = np.sin(angs).astype(np.float32)[None, :]
    p = np.arange(128)
    c["blk1"] = (p[:, None] // 64 == p[None, :] // 64).astype(np.float32)
    md = lambda b: (p[:, None] // b == p[None, :] // b).astype(np.float32)
    c["hmask"] = np.ascontiguousarray(np.stack([md(8), md(16) - md(8), md(32) - md(16), md(64) - md(32), md(128) - md(64)], 1))
    c["hsel"] = (p[:, None] // 64 == np.arange(2)[None, :]).astype(np.float32)
    return c


def prep_weights(inp, L):
    o = {}
    w_in = np.asarray(inp["w_in"])[:L][:, :, _COLS]
    o["w_in"] = np.ascontiguousarray(w_in.reshape(L, 8, 128, DIN).transpose(0, 2, 1, 3))
    o["w_out"] = np.ascontiguousarray(np.asarray(inp["w_out"])[:L].reshape(L, 8, 128, D).transpose(0, 2, 1, 3))
    o["ada_w"] = np.ascontiguousarray(np.asarray(inp["ada_w"])[:L].reshape(L, 8, 128, 3 * D).transpose(0, 2, 1, 3))
    o["ada_b"] = np.ascontiguousarray(np.asarray(inp["ada_b"])[:L])
    o["norm_g"] = np.ascontiguousarray(np.asarray(inp["norm_g"])[:L])

    def fm(v):
        return np.ascontiguousarray(np.asarray(v).reshape(-1, 2, 128).transpose(0, 2, 1))
    o["a_lb_logits"] = np.ascontiguousarray(np.asarray(inp["a_lb_logits"]).reshape(4, 2, 128).transpose(2, 1, 0))
    o["lbrow"] = np.ascontiguousarray(np.asarray(inp["a_lb_logits"]), dtype=np.float32)
    o["a_onorm_g"] = np.ascontiguousarray(np.asarray(inp["a_onorm_g"])[:L])
    mu = np.asarray(inp["b_mu"])[:L]
    mu_p = np.zeros((L, 128, 8), np.float32)
    for ch in range(6):
        mu_p[:, :, ch] = mu[:, ch * 128:(ch + 1) * 128]
    mu_p[:, :32, 6] = mu[:, 768:800]
    mu_p[:, :32, 7] = mu[:, 800:832]
    o["b_mu"] = mu_p
    o["b_mu_row"] = np.ascontiguousarray(mu)
    o["b_w0"] = fm(np.asarray(inp["b_w0"])[:L])
    o["b_a0"] = fm(np.asarray(inp["b_a0"])[:L])
    lo = np.zeros((L, 128, 512), np.float32)
    lo[:, :32, :] = np.concatenate([np.asarray(inp["b_w2"])[:L], np.asarray(inp["b_a2"])[:L]], 2)
    o["b_lora"] = lo
    o["b_k_k"] = fm(np.asarray(inp["b_k_k"])[:L])
    o["b_k_a"] = fm(np.asarray(inp["b_k_a"])[:L])
    o["b_r_k"] = fm(np.asarray(inp["b_r_k"])[:L])
    for k in ("b_w0", "b_a0", "b_k_k", "b_k_a", "b_r_k"):
        o[k + "_row"] = np.ascontiguousarray(np.asarray(inp[k])[:L])
    o["b_gn_w"] = np.ascontiguousarray(np.asarray(inp["b_gn_w"])[:L])
    o["b_gn_b"] = np.ascontiguousarray(np.asarray(inp["b_gn_b"])[:L])
    qk = np.concatenate([np.asarray(inp["c_qnorm_g"])[:L], np.asarray(inp["c_knorm_g"])[:L]], 1)
    o["c_qkg"] = np.ascontiguousarray(qk)
    o["d_w_pool"] = np.ascontiguousarray(np.asarray(inp["d_w_pool"])[:L].transpose(0, 2, 1, 3))
    o["d_scale"] = np.ascontiguousarray(np.asarray(inp["d_scale"])[:L])
    return o


def build(cfg):
    T = cfg["T"]; L = cfg["L"]; NS = cfg["NS"]
    mixers = cfg.get("mixers", "ABCD")
    dbg = cfg.get("dbg", False)
    NT = T // 128
    KEEP = min(2048, T)
    nc = bass.Bass("TRN2", target_bir_lowering=False)
    din = {}

    def inp(name, shape, dt=F32):
        din[name] = nc.dram_tensor(name, list(shape), dt, kind="ExternalInput").ap()
        return din[name]

    def outp(name, shape, dt=F32):
        return nc.dram_tensor(name, list(shape), dt, kind="ExternalOutput").ap()

    x_p = inp("x_p", [T, D]); c_pT = inp("c_pT", [128, 8])
    w_in = inp("w_in", [L, 128, 8, DIN]); w_out = inp("w_out", [L, 128, 8, D])
    ada_w = inp("ada_w", [L, 128, 8, 3 * D]); ada_b = inp("ada_b", [L, 3 * D]); norm_g = inp("norm_g", [L, D])
    a_lb = inp("a_lb_logits", [128, 2, 4]); a_onorm = inp("a_onorm_g", [L, 256])
    b_mu = inp("b_mu", [L, 128, 8]); b_w0 = inp("b_w0", [L, 128, 2]); b_a0 = inp("b_a0", [L, 128, 2])
    b_lora = inp("b_lora", [L, 128, 512]); b_k_k = inp("b_k_k", [L, 128, 2]); b_k_a = inp("b_k_a", [L, 128, 2])
    b_r_k = inp("b_r_k", [L, 128, 2]); b_gn_w = inp("b_gn_w", [L, 256]); b_gn_b = inp("b_gn_b", [L, 256])
    c_qkg = inp("c_qkg", [L, 128]); d_wp = inp("d_w_pool", [L, 64, 4, 64]); d_scale = inp("d_scale", [L, 256])
    k_ident = inp("ident", [128, 128]); k_mle = inp("m_le", [128, 128]); k_mlt = inp("m_lt", [128, 128])
    k_mgt = inp("m_gt", [128, 128]); k_amask = inp("amask", [128, 17, 128], BF16); k_identb = inp("identb", [128, 128], BF16); k_band = inp("band", [128, 12, 128])
    k_cos = inp("cos", [NT, 128, 32]); k_sin = inp("sin", [NT, 128, 32])
    k_blk1 = inp("blk1", [128, 128]); k_hsel = inp("hsel", [128, 2]); k_hmask = inp("hmask", [128, 5, 128])

    if NS:
        x_s = inp("x_s", [NS, D]); c_sT = inp("c_sT", [128, 8, NS])
        sA_in = inp("sA_in", [L, NS, 4, 64, 64]); sBw_in = inp("sBw_in", [L, NS, 4, 64, 64]); sBs_in = inp("sBs_in", [L, NS, BSW])
        ck_in = inp("ck_in", [L, NS, 2048, 256]); cv_in = inp("cv_in", [L, NS, 2048, 256]); cd_in = inp("cd_in", [L, NS, 15, 256])
        lbrow = inp("lbrow", [4, 256]); mu_row = inp("b_mu_row", [L, BSW])
        rowp = {k: inp(k + "_row", [L, 256]) for k in ("b_w0", "b_a0", "b_k_k", "b_k_a", "b_r_k")}
        k_cs = inp("cos_s", [1, 32]); k_sn = inp("sin_s", [1, 32])
        y_s = outp("y_s", [NS, D])
        oA_s = outp("oA_s", [L, NS, 4, 64, 64]); oBw_s = outp("oBw_s", [L, NS, 4, 64, 64]); oBs_s = outp("oBs_s", [L, NS, BSW])
        oK_s = outp("oK_s", [L, NS, 256]); oV_s = outp("oV_s", [L, NS, 256]); oD_s = outp("oD_s", [L, NS, 256])
    y_p = outp("y_p", [T, D])
    oA_p = outp("oA_p", [L, 4, 64, 64]); oBw_p = outp("oBw_p", [L, 4, 64, 64]); oBs_p = outp("oBs_p", [L, BSW])
    oK_p = outp("oK_p", [L, KEEP, 256]); oV_p = outp("oV_p", [L, KEEP, 256]); oD_p = outp("oD_p", [L, 15, 256])
    if dbg:
        mix_dbg = outp("mix_dbg", [T, D], BF16)

    P = Prog(nc)
    cnt = [0]

    def rr(*engs):
        cnt[0] += 1
        return engs[cnt[0] % len(engs)]

    with contextlib.ExitStack() as st:
        sblog = cfg.setdefault("_sblog", [])

        def sb(name, shape, dt=F32):
            sblog.append((name, int(np.prod(shape[1:])) * (2 if dt == BF16 else 4)))
            try:
                return st.enter_context(nc.sbuf_tensor("s_" + name, list(shape), dt))
            except AssertionError:
                tot = 0
                for n_, b_ in sorted(sblog, key=lambda x: -x[1]):
                    tot += b_
                    print(f"  {n_:12s} {b_:7d}  cum {tot}")
                raise

        def ps(name, shape, dt=F32):
            return st.enter_context(nc.psum_tensor("p_" + name, list(shape), dt))

        def load(dst_ap, src_ap, key, w):
            P.dma(lambda e: e.dma_start(out=dst_ap, in_=src_ap), key, w=w)

        def store(dst_ap, src_ap, key, r, w=()):
            P.dma(lambda e: e.dma_start(out=dst_ap, in_=src_ap), key, r=r, w=w)

        ident = sb("ident", [128, 128]); identb = sb("identb", [128, 128], BF16)
        m_le = sb("m_le", [128, 128]); m_lt = sb("m_lt", [128, 128]); m_gt = sb("m_gt", [128, 128])
        amask = sb("amask", [128, 17, 128], BF16)
        band = sb("band", [128, 12, 128])
        cos_t = sb("cos_t", [128, 32]); sin_t = sb("sin_t", [128, 32])
        blk1 = sb("blk1", [128, 128]); hsel = sb("hsel", [128, 2]); hmask = sb("hmask", [128, 5, 128])
        ones_t = sb("ones_t", [128, 128])
        for dst, src, nm in ((ident, k_ident, "ident"), (m_le, k_mle, "m_le"), (m_lt, k_mlt, "m_lt"), (m_gt, k_mgt, "m_gt"),
                             (amask, k_amask, "amask"), (identb, k_identb, "identb"), (band, k_band, "band"),
                             (blk1, k_blk1, "blk1"), (hsel, k_hsel, "hsel"), (hmask, k_hmask, "hmask")):
            load(dst[:], src, "const", [nm])
        P.op("pool", lambda e: e.memset(ones_t[:], 1.0), w=["ones_t"])

        win_b = sb("win_b", [128, 8, DIN], BF16); wout_b = sb("wout_b", [128, 8, D], BF16)
        big = [sb(f"big{i}", [128, 1024]) for i in range(2)]
        stg = [big[i][:].rearrange("p (a b) -> p a b", b=128) for i in range(2)]
        stg_i = [0]

        def stage(src_ap, ncols, wide=False):
            if wide:
                i = stg_i[0] % 4
                stg_i[0] += 1
                buf, key = stg4[i]
                load(buf[:, :, 0:ncols], src_ap, f"stg{i}", [key])
                return buf, key
            i = stg_i[0] % 2
            stg_i[0] += 1
            load(stg[i][:, :, 0:ncols], src_ap, f"stg{i}", [f"big{i}"])
            return stg[i], f"big{i}"

        def load_layer_weights(l):
            for j in range(29):
                n = min(128, DIN - j * 128)
                s, k = stage(w_in[l, :, :, j * 128:j * 128 + n], n, wide=True)
                eng = rr("act", "pool")
                if eng == "act":
                    P.op("act", lambda e, s=s, j=j, n=n: e.copy(out=win_b[:, :, j * 128:j * 128 + n], in_=s[:, :, 0:n]), r=[k], w=["win_b"])
                else:
                    P.op("pool", lambda e, s=s, j=j, n=n: e.tensor_copy(out=win_b[:, :, j * 128:j * 128 + n], in_=s[:, :, 0:n]), r=[k], w=["win_b"])
            for j in range(8):
                s, k = stage(w_out[l, :, :, j * 128:(j + 1) * 128], 128, wide=True)
                P.op("pool", lambda e, s=s, j=j: e.tensor_copy(out=wout_b[:, :, j * 128:(j + 1) * 128], in_=s[:, :, :]), r=[k], w=["wout_b"])

        xt0_ = sb("xt0", [128, D])
        xt = [xt0_, xt0_]
        h1 = sb("h1", [128, D])
        stg4 = [(stg[0], "big0"), (stg[1], "big1"), (xt0_[:].rearrange("p (a b) -> p a b", b=128), "xt0"), (h1[:].rearrange("p (a b) -> p a b", b=128), "h1")]
        gs_bc = sb("gs_bc", [128, D]); shift_bc = sb("shift_bc", [128, D]); gate_bc = sb("gate_bc", [128, D])
        adab_pc = sb("adab_pc", [128, 128])
        scT = sb("scT", [128, 8])
        onorm_bc = sb("onorm_bc", [128, 256]); gnw_bc = sb("gnw_bc", [128, 256]); gnb_bc = sb("gnb_bc", [128, 256])
        qkg_bc = sb("qkg_bc", [128, 128]); dsc_bc = sb("dsc_bc", [64, 256]); wpool = sb("wpool", [128, 4, 64])
        P.op("pool", lambda e: e.memset(wpool[:], 0.0), w=["wpool"])
        lbl = sb("lbl", [128, 2, 4]); lb_all = sb("lb_all", [128, 2, 4]); omlb_all = sb("omlb_all", [128, 2, 4])
        mu_t = sb("mu_t", [128, 8]); w0_t = sb("w0_t", [128, 2]); a0_t = sb("a0_t", [128, 2]); lora_t = sb("lora_t", [128, 512])
        kk_t = sb("kk_t", [128, 2]); ka_t = sb("ka_t", [128, 2]); omka_t = sb("omka_t", [128, 2]); rk_t = sb("rk_t", [128, 2])

        psP = [ps(f"psP{i}", [128, 512]) for i in range(3)]
        psT = ps("psT", [128, 8, 128], BF16)
        psS = [ps(f"psS{i}", [128, 512]) for i in range(2)]
        psO = ps("psO", [128, 512])
        psM = ps("psM", [128, 512])
        pP_i = [0]
        inv_i = [0]

        def next_psP():
            i = pP_i[0] % 3
            pP_i[0] += 1
            return psP[i], f"psP{i}"

        load(lbl[:], a_lb, "lbl", ["lbl"])
        lsum = sb("lsum", [128, 2])
        P.op("act", lambda e: e.activation(out=lbl[:], in_=lbl[:], func=AF.Exp), r=["lbl"], w=["lbl"])
        P.op("dve", lambda e: e.tensor_reduce(out=lsum[:], in_=lbl[:], axis=AX.X, op=ALU.add), r=["lbl"], w=["lsum"])
        P.op("dve", lambda e: e.reciprocal(out=lsum[:], in_=lsum[:]), r=["lsum"], w=["lsum"])
        P.op("dve", lambda e: e.tensor_tensor(out=lbl[:], in0=lbl[:], in1=lsum[:].unsqueeze(2).to_broadcast([128, 2, 4]), op=ALU.mult), r=["lbl", "lsum"], w=["lbl"])
        P.op("pool", lambda e: e.memset(lb_all[:], 0.0), w=["lb_all"])
        for l in range(1, 4):
            P.op("dve", lambda e, l=l: e.tensor_tensor(out=lb_all[:, :, l], in0=lb_all[:, :, l - 1], in1=lbl[:, :, l], op=ALU.add), r=["lb_all", "lbl"], w=["lb_all"])
        P.op("dve", lambda e: e.tensor_scalar(out=omlb_all[:], in0=lb_all[:], scalar1=-1.0, scalar2=1.0, op0=ALU.mult, op1=ALU.add), r=["lb_all"], w=["omlb_all"])

        load(scT[:], c_pT, "scT", ["scT"])
        P.op("act", lambda e: e.activation(out=scT[:], in_=scT[:], func=AF.Silu), r=["scT"], w=["scT"])

        def layer_setup(l, sample=False):
            NP_ = NS if sample else 128
            normg_bc = h1
            load(normg_bc[:], norm_g[l:l + 1, :].partition_broadcast(128), "normg", ["h1"])
            if sample:
                scTb = scsT
            else:
                scTb = xt[0][:].rearrange("p (a b) -> p a b", b=128)
                P.op("dve", lambda e: e.tensor_copy(out=scTb, in_=scT[:].unsqueeze(2).to_broadcast([128, 8, 128])), r=["scT"], w=["xt0"])
            for j in range(24):
                s, k = stage(ada_w[l, :, :, j * 128:(j + 1) * 128], 128)
                load(adab_pc[:], ada_b[l:l + 1, j * 128:(j + 1) * 128].partition_broadcast(128), "adab", ["adab_pc"])
                pp, pk = next_psP()
                for kc in range(8):
                    P.op("pe", lambda e, s=s, kc=kc, pp=pp: e.matmul(pp[0:NP_, 0:128], lhsT=scTb[:, kc, :], rhs=s[:, kc, :], start=(kc == 0), stop=(kc == 7)),
                         r=["xt0", "qT", k], w=[pk])
                dst, dk = ((shift_bc, "shift_bc"), (gs_bc, "gs_bc"), (gate_bc, "gate_bc"))[j // 8]
                c0 = (j % 8) * 128
                P.op("dve", lambda e, pp=pp, dst=dst, c0=c0: e.tensor_tensor(out=dst[0:NP_, c0:c0 + 128], in0=pp[0:NP_, 0:128], in1=adab_pc[0:NP_, :], op=ALU.add),
                     r=[pk, "adab_pc"], w=[dk])
            P.op("dve", lambda e: e.scalar_tensor_tensor(out=gs_bc[0:NP_, :], in0=gs_bc[0:NP_, :], scalar=1.0, in1=normg_bc[0:NP_, :], op0=ALU.add, op1=ALU.mult), r=["gs_bc", "h1"], w=["gs_bc"])
            load(onorm_bc[:], a_onorm[l:l + 1, :].partition_broadcast(128), "lw1", ["onorm_bc"])
            load(gnw_bc[:], b_gn_w[l:l + 1, :].partition_broadcast(128), "lw2", ["gnw_bc"])
            load(gnb_bc[:], b_gn_b[l:l + 1, :].partition_broadcast(128), "lw3", ["gnb_bc"])
            load(qkg_bc[:], c_qkg[l:l + 1, :].partition_broadcast(128), "lw4", ["qkg_bc"])
            load(dsc_bc[:], d_scale[l:l + 1, :].partition_broadcast(64), "lw5", ["dsc_bc"])
            load(wpool[0:64], d_wp[l], "lw6", ["wpool"])
            P.op("dve", lambda e: e.tensor_tensor(out=wpool[0:64], in0=wpool[0:64], in1=dsc_bc[:].rearrange("p (g d) -> p g d", g=4), op=ALU.mult), r=["wpool", "dsc_bc"], w=["wpool"])
            load(mu_t[:], b_mu[l], "lw7", ["mu_t"]); load(w0_t[:], b_w0[l], "lw8", ["w0_t"]); load(a0_t[:], b_a0[l], "lw9", ["a0_t"])
            load(lora_t[:], b_lora[l], "lw10", ["lora_t"]); load(kk_t[:], b_k_k[l], "lw11", ["kk_t"]); load(ka_t[:], b_k_a[l], "lw12", ["ka_t"])
            load(rk_t[:], b_r_k[l], "lw13", ["rk_t"])
            P.op("dve", lambda e: e.tensor_scalar(out=omka_t[:], in0=ka_t[:], scalar1=-1.0, scalar2=1.0, op0=ALU.mult, op1=ALU.add), r=["ka_t"], w=["omka_t"])

        ssq = sb("ssq", [128, 1]); rstd = sb("rstd", [128, 1])
        hb = sb("hb", [128, D], BF16); hT = sb("hT", [128, 8, 128], BF16)
        iu = sb("iu", [128, 2, 512])
        sg = sb("sg", [128, 1024])
        mix = sb("mix", [128, D], BF16); mixT = hT
        xo = h1
        NR = 17
        kT_hist = sb("kT_hist", [128, 2, NR * 128], BF16); v_hist = sb("v_hist", [128, NR, 4, 66], BF16)
        qT = sb("qT", [128, 2, 128], BF16)
        qk_ss = sb("qk_ss", [128, 8])
        qkrb = sb("qkrb", [128, 512], BF16)
        pexp = [sb(f"pexp{i}", [128, 4, 128], BF16) for i in range(2)]
        rden = sb("rden", [128, 4])
        P.op("pool", lambda e: e.memset(v_hist[:], 1.0), w=["v_hist"])

        def tile_front(l, i):
            X = xt[0]; xk = "xt0"
            src = x_p if l == 0 else y_p
            P.dma(lambda e: e.dma_start(out=X[:], in_=src[i * 128:(i + 1) * 128, :]), xk, r=[f"y{i}"], w=[xk])
            P.op("act", lambda e: e.activation(out=h1[:], in_=X[:], func=AF.Square, accum_out=ssq[:]), r=[xk], w=["h1", "ssq"])
            P.op("act", lambda e: e.activation(out=rstd[:], in_=ssq[:], func=AF.Ln, scale=1.0 / D, bias=EPS), r=["ssq"], w=["rstd"])
            P.op("act", lambda e: e.activation(out=rstd[:], in_=rstd[:], func=AF.Exp, scale=-0.5), r=["rstd"], w=["rstd"])
            P.op("dve", lambda e: e.scalar_tensor_tensor(out=h1[:], in0=X[:], scalar=rstd[:, 0:1], in1=gs_bc[:], op0=ALU.mult, op1=ALU.mult),
                 r=[xk, "rstd", "gs_bc"], w=["h1"])
            P.op("dve", lambda e: e.tensor_tensor(out=hb[:], in0=h1[:], in1=shift_bc[:], op=ALU.add), r=["h1", "shift_bc"], w=["hb"])
            for kc in range(8):
                P.op("pe", lambda e, kc=kc: e.transpose(out=psT[:, kc, :], in_=hb[:, kc * 128:(kc + 1) * 128], identity=identb[:]), r=["hb", "identb"], w=["psT"])
            P.op("act", lambda e: e.copy(out=hT[:], in_=psT[:]), r=["psT"], w=["hT"])
            return X, xk

        def proj_fm(chunks):
            pp, pk = next_psP()
            for j, (c0, n) in enumerate(chunks):
                for kc in range(8):
                    P.op("pe", lambda e, j=j, c0=c0, n=n, kc=kc, pp=pp: e.matmul(pp[0:n, j * 128:(j + 1) * 128], lhsT=win_b[:, kc, c0:c0 + n], rhs=hT[:, kc, :],
                                                                                start=(kc == 0), stop=(kc == 7)), r=["win_b", "hT"], w=[pk])
            return pp, pk

        def proj_tm(c0, n):
            pp, pk = next_psP()
            for kc in range(8):
                P.op("pe", lambda e, kc=kc, pp=pp: e.matmul(pp[:, 0:n], lhsT=hT[:, kc, :], rhs=win_b[:, kc, NFM + c0:NFM + c0 + n], start=(kc == 0), stop=(kc == 7)),
                     r=["win_b", "hT"], w=[pk])
            return pp, pk

        def mixer_D(l, i):
            pooledT = big[1][0:64, 512:1024].rearrange("p (g t) -> p g t", t=128)
            cur = iu[:, i % 2, 256:512]; prv = iu[:, (i + 1) % 2, 256:512]
            ck = f"iu{i % 2}"; pvk = f"iu{(i + 1) % 2}"
            bsel = 0 if i == 0 else 4
            for g in range(4):
                P.op("pe", lambda e, g=g: e.matmul(psM[0:64, g * 128:(g + 1) * 128], lhsT=cur[:, g * 64:(g + 1) * 64], rhs=band[:, bsel + g, :], start=True, stop=(i == 0)),
                     r=[ck, "band"], w=["psM"])
                if i > 0:
                    P.op("pe", lambda e, g=g: e.matmul(psM[0:64, g * 128:(g + 1) * 128], lhsT=prv[:, g * 64:(g + 1) * 64], rhs=band[:, 8 + g, :], start=False, stop=True),
                         r=[pvk, "band"], w=["psM"])
            P.op("act", lambda e: e.copy(out=big[1][0:64, 512:1024], in_=psM[0:64, :]), r=["psM"], w=["big1"])
            for g in range(4):
                P.op("pe", lambda e, g=g: e.matmul(psO[:, g * 64:(g + 1) * 64], lhsT=pooledT[:, g, :], rhs=wpool[0:64, g, :], start=True, stop=True),
                     r=["big1", "wpool"], w=["psO"])
            P.op("dve", lambda e: e.tensor_tensor(out=mix[:, 768:1024], in0=psO[:, 0:256], in1=sg[:, 768:1024], op=ALU.mult), r=["psO", "sg"], w=["mixD"])
            if i == NT - 1:
                store(oD_p[l], iu[113:128, i % 2, 256:512], "oD", r=[ck])

        def mixer_C(l, i):
            qk_sq = big[0][:, 0:512]; qkn = big[0][:, 512:1024]; qkr = big[1][:, 0:512]
            rp_a = W("b5", [128, 2, 128])[:].rearrange("p c (a b) -> p (c a) b", b=32); rp_b = W("b6", [128, 2, 128])[:].rearrange("p c (a b) -> p (c a) b", b=32)
            vf = W("b7", [128, 2, 128])[:].rearrange("p c t -> p (c t)")
            pp, pk = proj_tm(512, 512)
            P.op("act", lambda e: e.activation(out=qk_sq, in_=pp[:], func=AF.Square), r=[pk], w=["big0"])
            P.op("dve", lambda e: e.tensor_reduce(out=qk_ss[:], in_=qk_sq.rearrange("p (h d) -> p h d", d=64), axis=AX.X, op=ALU.add), r=["big0"], w=["qk_ss"])
            P.op("act", lambda e: e.activation(out=qk_ss[:], in_=qk_ss[:], func=AF.Ln, scale=1.0 / HD, bias=EPS), r=["qk_ss"], w=["qk_ss"])
            P.op("act", lambda e: e.activation(out=qk_ss[:], in_=qk_ss[:], func=AF.Exp, scale=-0.5), r=["qk_ss"], w=["qk_ss"])
            P.op("dve", lambda e: e.tensor_tensor(out=qkn.rearrange("p (h d) -> p h d", d=64), in0=pp[:].rearrange("p (h d) -> p h d", d=64),
                                                   in1=qk_ss[:].unsqueeze(2).to_broadcast([128, 8, 64]), op=ALU.mult), r=[pk, "qk_ss"], w=["big0"])
            P.op("pool", lambda e: e.tensor_tensor(out=qkn.rearrange("p (a h d) -> p a h d", a=2, d=64), in0=qkn.rearrange("p (a h d) -> p a h d", a=2, d=64),
                                                    in1=qkg_bc[:].rearrange("p (a d) -> p a d", d=64).unsqueeze(2).to_broadcast([128, 2, 4, 64]), op=ALU.mult), r=["big0", "qkg_bc"], w=["big0"])
            q3 = qkn.rearrange("p (h d) -> p h d", d=64); o3 = qkr.rearrange("p (h d) -> p h d", d=64)
            load(cos_t[:], k_cos[i], "cos_t", ["cos_t"]); load(sin_t[:], k_sin[i], "sin_t", ["sin_t"])
            cs = cos_t[:].unsqueeze(1).to_broadcast([128, 8, 32]); sn = sin_t[:].unsqueeze(1).to_broadcast([128, 8, 32])
            P.op("dve", lambda e: e.tensor_tensor(out=rp_a, in0=q3[:, :, 0:32], in1=cs, op=ALU.mult), r=["big0", "cos_t"], w=["b5"])
            P.op("pool", lambda e: e.tensor_tensor(out=rp_b, in0=q3[:, :, 32:64], in1=sn, op=ALU.mult), r=["big0", "sin_t"], w=["b6"])
            P.op("dve", lambda e: e.tensor_tensor(out=o3[:, :, 0:32], in0=rp_a, in1=rp_b, op=ALU.subtract), r=["b5", "b6"], w=["big1"])
            P.op("dve", lambda e: e.tensor_tensor(out=rp_a, in0=q3[:, :, 32:64], in1=cs, op=ALU.mult), r=["big0", "cos_t"], w=["b5"])
            P.op("pool", lambda e: e.tensor_tensor(out=rp_b, in0=q3[:, :, 0:32], in1=sn, op=ALU.mult), r=["big0", "sin_t"], w=["b6"])
            P.op("dve", lambda e: e.tensor_tensor(out=o3[:, :, 32:64], in0=rp_a, in1=rp_b, op=ALU.add), r=["b5", "b6", "big1"], w=["big1"])
            P.op("act", lambda e: e.copy(out=qkrb[:], in_=qkr), r=["big1"], w=["qkrb"])
            if cfg.get("cut") == 1:
                return
            pv, pvk = proj_tm(1024, 256)
            P.op("act", lambda e: e.copy(out=vf, in_=pv[:, 0:256]), r=[pvk], w=["b7"])
            P.op("dve", lambda e: e.tensor_copy(out=v_hist[:, i % NR, :, 0:64], in_=pv[:, 0:256].rearrange("p (h d) -> p h d", d=64)), r=[pvk], w=["v_hist"])
            t0 = i * 128 - (T - KEEP)
            if t0 >= 0:
                store(oK_p[l, t0:t0 + 128, :], qkr[:, 256:512], "oK", r=["big1"])
                store(oV_p[l, t0:t0 + 128, :], vf, "oV", r=["b7"])
            if cfg.get("cut") == 2:
                return
            for j in range(4):
                P.op("pe", lambda e, j=j: e.transpose(out=psT[:, j, :], in_=qkrb[:, j * 128:(j + 1) * 128], identity=identb[:]), r=["qkrb", "identb"], w=["psT"])
            P.op("act", lambda e: e.copy(out=qT[:], in_=psT[:, 0:2, :]), r=["psT"], w=["qT"])
            P.op("act", lambda e: e.copy(out=kT_hist[:, :, (i % NR) * 128:(i % NR + 1) * 128], in_=psT[:, 2:4, :]), r=["psT"], w=["kT_hist"])
            if cfg.get("cut") == 3:
                return
            k_lo = max(0, i - 16)
            groups = [list(range(a, min(a + 4, i + 1))) for a in range(k_lo, i + 1, 4)]
            gi = 0
            pend = None

            def emit_pv(h, grp, PE_, pek):
                for j, kj in enumerate(grp):
                    P.op("pe", lambda e, j=j, kj=kj, PE_=PE_, h=h: e.matmul(psO[:, h * 128:h * 128 + 66], lhsT=PE_[:, j, :], rhs=v_hist[:, kj % NR, h, :], start=(kj == k_lo), stop=(kj == i)),
                         r=[pek, "v_hist"], w=["psO"])
            for h in range(4):
                c = h // 2; pb = 64 * (h % 2)
                for grp in groups:
                    S = psS[gi % 2]; sk = f"psS{gi % 2}"; PE_ = pexp[gi % 2]; pek = f"pexp{gi % 2}"
                    gi += 1
                    n = len(grp)
                    for j, kj in enumerate(grp):
                        P.op("pe", lambda e, j=j, kj=kj, S=S, pb=pb, c=c: e.matmul(S[:, j * 128:(j + 1) * 128], lhsT=kT_hist[pb:pb + 64, c, (kj % NR) * 128:(kj % NR + 1) * 128], rhs=qT[pb:pb + 64, c, :],
                                                                    start=True, stop=True), r=["kT_hist", "qT"], w=[sk])
                    if pend is not None:
                        emit_pv(*pend)
                    P.op("act", lambda e, S=S, PE_=PE_, n=n: e.activation(out=PE_[:, 0:n, :].rearrange("p a t -> p (a t)"), in_=S[:, 0:n * 128], func=AF.Exp, scale=HD ** -0.5),
                         r=[sk], w=[pek])
                    r0 = 16 - (i - grp[0])
                    P.op("dve", lambda e, PE_=PE_, n=n, r0=r0: e.tensor_tensor(out=PE_[:, 0:n, :], in0=PE_[:, 0:n, :], in1=amask[:, r0:r0 + n, :], op=ALU.mult),
                         r=[pek, "amask"], w=[pek])
                    pend = (h, grp, PE_, pek)
            emit_pv(*pend)
            o4 = psO[:].rearrange("p (h d) -> p h d", d=128)
            P.op("dve", lambda e: e.reciprocal(out=rden[:], in_=o4[:, :, 64]), r=["psO"], w=["rden"])
            oc = W("o_sb", [128, 256])
            P.op("dve", lambda e: e.tensor_tensor(out=oc[:].rearrange("p (h d) -> p h d", d=64), in0=o4[:, :, 0:64], in1=rden[:].unsqueeze(2).to_broadcast([128, 4, 64]), op=ALU.mult),
                 r=["psO", "rden"], w=["o_sb"])
            P.op("dve", lambda e: e.tensor_tensor(out=mix[:, 512:768], in0=oc[:], in1=sg[:, 512:768], op=ALU.mult), r=["o_sb", "sg"], w=["mixC"])

        wcache = {}

        def W(name, shape, dt=F32):
            if name not in wcache:
                wcache[name] = sb("w_" + name, shape, dt)
            return wcache[name]

        S_A = sb("S_A", [128, 2, 64])

        def mixer_A(l, i):
            pp, pk = proj_fm([(0, 128), (128, 128), (256, 128), (384, 128)])
            qs = W("qs", [128, 2, 128]); fg = W("fg", [128, 2, 128]); lf = W("lf", [128, 2, 128]); kk = W("kk", [128, 2, 128])
            G = W("G", [128, 2, 128]); t1 = W("t1", [128, 2, 128]); t2 = W("t2", [128, 2, 128])
            Qin = W("Qin", [128, 2, 128]); Qoff = W("Qoff", [128, 2, 128]); Qd = W("Qd", [128, 2, 128]); Kd = W("Kd", [128, 2, 128])
            Ka = W("Ka", [128, 2, 3, 96]); Kh = W("Kh", [128, 2, 128]); eG = W("eG", [128, 2]); Khtok = W("Khtok", [128, 256])
            attT = W("attT", [128, 4, 128]); o_sb = W("o_sb", [128, 256]); osq = W("osq", [128, 256]); oss = W("oss", [128, 4])
            V = iu[:, i % 2, 0:256]; vk = f"iu{i % 2}"
            P.op("act", lambda e: e.activation(out=qs[:].rearrange("p c t -> p (c t)"), in_=pp[:, 0:256], func=AF.Silu), r=[pk], w=["qs"])
            P.op("act", lambda e: e.activation(out=fg[:].rearrange("p c t -> p (c t)"), in_=pp[:, 256:512], func=AF.Sigmoid), r=[pk], w=["fg"])
            for c in range(2):
                P.op("dve", lambda e, c=c: e.tensor_scalar(out=fg[:, c, :], in0=fg[:, c, :], scalar1=omlb_all[:, c, l:l + 1], scalar2=lb_all[:, c, l:l + 1], op0=ALU.mult, op1=ALU.add),
                     r=["fg", "omlb_all", "lb_all"], w=["fg"])
            if cfg.get("cut") == 1:
                return
            P.op("act", lambda e: e.activation(out=lf[:], in_=fg[:], func=AF.Ln), r=["fg"], w=["lf"])
            P.op("pool", lambda e: e.tensor_scalar(out=kk[:], in0=fg[:], scalar1=-1.0, scalar2=1.0, op0=ALU.mult, op1=ALU.add), r=["fg"], w=["kk"])
            for c in range(2):
                P.op("dve", lambda e, c=c: e.tensor_tensor_scan(out=G[:, c, :], data0=ones_t[:], data1=lf[:, c, :], initial=0.0, op0=ALU.mult, op1=ALU.add),
                     r=["ones_t", "lf"], w=["G"])
            if cfg.get("cut") == 2:
                return
            P.op("act", lambda e: e.activation(out=t1[:], in_=G[:], func=AF.Exp), r=["G"], w=["t1"])
            P.op("pool", lambda e: e.tensor_tensor(out=Qin[:], in0=qs[:], in1=t1[:], op=ALU.mult), r=["qs", "t1"], w=["Qin"])
            g4 = lambda t: t[:, :, 32:128].rearrange("p c (a b) -> p c a b", b=32)
            P.op("dve", lambda e: e.tensor_tensor(out=g4(t2), in0=g4(G), in1=G[:, :, 31:127:32].unsqueeze(3).to_broadcast([128, 2, 3, 32]), op=ALU.subtract), r=["G"], w=["t2"])
            P.op("act", lambda e: e.activation(out=t2[:, :, 32:128], in_=t2[:, :, 32:128], func=AF.Exp), r=["t2"], w=["t2"])
            P.op("dve", lambda e: e.tensor_tensor(out=Qoff[:, :, 32:128], in0=qs[:, :, 32:128], in1=t2[:, :, 32:128], op=ALU.mult), r=["qs", "t2"], w=["Qoff"])
            for a in range(1, 4):
                for c in range(2):
                    P.op("act", lambda e, a=a, c=c: e.activation(out=Ka[:, c, a - 1, 0:32 * a], in_=G[:, c, 0:32 * a], func=AF.Exp, scale=-1.0, bias=G[:, c, 32 * a - 1:32 * a]),
                         r=["G"], w=["Ka"])
                P.op("pool", lambda e, a=a: e.tensor_tensor(out=Ka[:, :, a - 1, 0:32 * a], in0=Ka[:, :, a - 1, 0:32 * a], in1=kk[:, :, 0:32 * a], op=ALU.mult), r=["Ka", "kk"], w=["Ka"])
            d4 = lambda t: t[:].rearrange("p c (a b) -> p c a b", b=32)
            P.op("dve", lambda e: e.tensor_tensor(out=d4(t1), in0=d4(G), in1=G[:, :, 15:128:32].unsqueeze(3).to_broadcast([128, 2, 4, 32]), op=ALU.subtract), r=["G", "t1"], w=["t1"])
            P.op("act", lambda e: e.activation(out=Qd[:], in_=t1[:], func=AF.Exp), r=["t1"], w=["Qd"])
            P.op("act", lambda e: e.activation(out=Kd[:], in_=t1[:], func=AF.Exp, scale=-1.0), r=["t1"], w=["Kd"])
            P.op("dve", lambda e: e.tensor_tensor(out=Qd[:], in0=Qd[:], in1=qs[:], op=ALU.mult), r=["Qd", "qs"], w=["Qd"])
            P.op("pool", lambda e: e.tensor_tensor(out=Kd[:], in0=Kd[:], in1=kk[:], op=ALU.mult), r=["Kd", "kk"], w=["Kd"])
            for c in range(2):
                P.op("act", lambda e, c=c: e.activation(out=Kh[:, c, :], in_=G[:, c, :], func=AF.Exp, scale=-1.0, bias=G[:, c, 127:128]), r=["G"], w=["Kh"])
            P.op("dve", lambda e: e.tensor_tensor(out=Kh[:], in0=Kh[:], in1=kk[:], op=ALU.mult), r=["Kh", "kk"], w=["Kh"])
            P.op("act", lambda e: e.activation(out=eG[:], in_=G[:, :, 127], func=AF.Exp), r=["G"], w=["eG"])
            if cfg.get("cut") == 3:
                return
            att = psS[0][:].rearrange("p (h t) -> p h t", t=128)
            attd = psS[1][:].rearrange("p (h t) -> p h t", t=128)
            for h in range(4):
                c = h // 2; pb = 64 * (h % 2)
                for a in range(1, 4):
                    P.op("pe", lambda e, a=a, h=h, c=c, pb=pb: e.matmul(att[0:32 * a, h, 32 * a:32 * a + 32], lhsT=Ka[pb:pb + 64, c, a - 1, 0:32 * a], rhs=Qoff[pb:pb + 64, c, 32 * a:32 * a + 32],
                                                                      start=True, stop=True, tile_position=(pb, 0)), r=["Ka", "Qoff"], w=["psS0"])
                P.op("pe", lambda e, h=h, c=c, pb=pb: e.matmul(attd[:, h, :], lhsT=Kd[pb:pb + 64, c, :], rhs=Qd[pb:pb + 64, c, :], start=True, stop=True), r=["Kd", "Qd"], w=["psS1"])
            if cfg.get("cut") in (31, 32):
                return
            P.op("pool", lambda e: e.memset(attT[:], 0.0), w=["attT"])
            for a in range(1, 4):
                n = 32 * a
                P.op("dve", lambda e, a=a, n=n: e.tensor_copy(out=attT[0:n, :, 32 * a:32 * a + 32], in_=att[0:n, :, 32 * a:32 * a + 32]), r=["psS0", "attT"], w=["attT"])
            for a in range(4):
                lo = 32 * a
                P.op("dve", lambda e, lo=lo: e.copy_predicated(out=attT[lo:lo + 32, :, lo:lo + 32], mask=m_le[lo:lo + 32, lo:lo + 32].bitcast(U32).unsqueeze(1).to_broadcast([32, 4, 32]),
                                                               data=attd[lo:lo + 32, :, lo:lo + 32]), r=["psS1", "m_le", "attT"], w=["attT"])
            if cfg.get("cut") == 4:
                return
            o4 = psO[:, 0:256].rearrange("p (h v) -> p h v", v=64)
            for h in range(4):
                c = h // 2; pb = 64 * (h % 2)
                P.op("pe", lambda e, h=h, c=c, pb=pb: e.matmul(o4[:, h, :], lhsT=Qin[pb:pb + 64, c, :], rhs=S_A[pb:pb + 64, c, :], start=True, stop=False), r=["Qin", "S_A"], w=["psO"])
                P.op("pe", lambda e, h=h: e.matmul(o4[:, h, :], lhsT=attT[:, h, :], rhs=V[:, h * 64:(h + 1) * 64], start=False, stop=True), r=["attT", vk], w=["psO"])
            P.op("act", lambda e: e.copy(out=o_sb[:], in_=psO[:, 0:256]), r=["psO"], w=["o_sb"])
            P.op("dve", lambda e: e.tensor_tensor(out=osq[:], in0=o_sb[:], in1=o_sb[:], op=ALU.mult), r=["o_sb"], w=["osq"])
            P.op("dve", lambda e: e.tensor_reduce(out=oss[:], in_=osq[:].rearrange("p (h v) -> p h v", v=64), axis=AX.X, op=ALU.add), r=["osq"], w=["oss"])
            P.op("act", lambda e: e.activation(out=oss[:], in_=oss[:], func=AF.Ln, scale=1.0 / HD, bias=EPS), r=["oss"], w=["oss"])
            P.op("act", lambda e: e.activation(out=oss[:], in_=oss[:], func=AF.Exp, scale=-0.5), r=["oss"], w=["oss"])
            P.op("dve", lambda e: e.tensor_tensor(out=o_sb[:].rearrange("p (h v) -> p h v", v=64), in0=o_sb[:].rearrange("p (h v) -> p h v", v=64),
                                                   in1=oss[:].unsqueeze(2).to_broadcast([128, 4, 64]), op=ALU.mult), r=["o_sb", "oss"], w=["o_sb"])
            P.op("dve", lambda e: e.tensor_tensor(out=o_sb[:], in0=o_sb[:], in1=onorm_bc[:], op=ALU.mult), r=["o_sb", "onorm_bc"], w=["o_sb"])
            P.op("dve", lambda e: e.tensor_tensor(out=mix[:, 0:256], in0=o_sb[:], in1=sg[:, 0:256], op=ALU.mult), r=["o_sb", "sg"], w=["mixA"])
            if cfg.get("cut") == 5:
                return
            for c in range(2):
                P.op("pe", lambda e, c=c: e.transpose(out=psM[:, c * 128:(c + 1) * 128], in_=Kh[:, c, :], identity=ident[:]), r=["Kh", "ident"], w=["psM"])
            P.op("act", lambda e: e.copy(out=Khtok[:], in_=psM[:, 0:256]), r=["psM"], w=["Khtok"])
            for c in range(2):
                P.op("pe", lambda e, c=c: e.matmul(psS[1][:, c * 128:(c + 1) * 128], lhsT=Khtok[:, c * 128:(c + 1) * 128], rhs=V[:, c * 128:(c + 1) * 128], start=True, stop=True),
                     r=["Khtok", vk], w=["psS1"])
            for c in range(2):
                for hl in range(2):
                    pb = 64 * hl
                    P.op("dve", lambda e, c=c, pb=pb, hl=hl: e.scalar_tensor_tensor(out=S_A[pb:pb + 64, c, :], in0=S_A[pb:pb + 64, c, :], scalar=eG[pb:pb + 64, c:c + 1],
                                                                                  in1=psS[1][pb:pb + 64, c * 128 + hl * 64:c * 128 + (hl + 1) * 64], op0=ALU.mult, op1=ALU.add),
                         r=["S_A", "eG", "psS1"], w=["S_A"])
            if i == NT - 1:
                store(oA_p[l].rearrange("(c hl) k v -> (hl k) c v", hl=2), S_A[:], "oA", r=["S_A"])

        bc4 = lambda m: m[:].unsqueeze(1).to_broadcast([128, 4, 128])
        Z_B = sb("Z_B", [128, 2, 2, 64]); XS = sb("XS", [128, 8, 129]); xm = sb("xm", [128, 8, 128])
        P.op("pool", lambda e: e.memset(XS[:], 0.0), w=["XS"])
        P.op("pool", lambda e: e.memset(xm[:], 0.0), w=["xm"])
        CW = 0.6065306597126334

        def mixer_B(l, i):
            pp1, pk1 = proj_fm([(512, 128), (640, 128), (768, 128), (896, 128)])
            P.op("act", lambda e: e.copy(out=XS[:, 0:4, 1:129], in_=pp1[:].rearrange("p (j t) -> p j t", t=128)), r=[pk1], w=["XS"])
            pp2, pk2 = proj_fm([(1024, 128), (1152, 128), (1280, 32), (1312, 32)])
            P.op("act", lambda e: e.copy(out=XS[:, 4:6, 1:129], in_=pp2[:, 0:256].rearrange("p (j t) -> p j t", t=128)), r=[pk2], w=["XS"])
            P.op("act", lambda e: e.copy(out=XS[0:32, 6:8, 1:129], in_=pp2[0:32, 256:512].rearrange("p (j t) -> p j t", t=128)), r=[pk2], w=["XS"])
            P.op("dve", lambda e: e.tensor_tensor(out=xm[:], in0=XS[:, :, 0:128], in1=XS[:, :, 1:129], op=ALU.subtract), r=["XS"], w=["xm"])
            P.op("dve", lambda e: e.tensor_tensor(out=xm[:], in0=xm[:], in1=mu_t[:].unsqueeze(2).to_broadcast([128, 8, 128]), op=ALU.mult), r=["xm", "mu_t"], w=["xm"])
            P.op("dve", lambda e: e.tensor_tensor(out=xm[:], in0=xm[:], in1=XS[:, :, 1:129], op=ALU.add), r=["xm", "XS"], w=["xm"])
            P.op("pool", lambda e: e.tensor_copy(out=XS[:, :, 0], in_=XS[:, :, 128]), r=["XS"], w=["XS"])
            r_ = xm[:, 0:2, :]; k_ = xm[:, 2:4, :]; v_ = xm[:, 4:6, :]
            sw = W("qs", [128, 2, 128]); av = W("fg", [128, 2, 128]); G = W("G", [128, 2, 128]); Gx = W("t1", [128, 2, 128])
            kkn = W("lf", [128, 2, 128]); k2 = W("kk", [128, 2, 128]); bvec = W("t2", [128, 2, 128]); tmp = W("Qin", [128, 2, 128])
            B1 = W("Qoff", [128, 2, 128]); B2 = W("Qd", [128, 2, 128]); B3 = W("Kd", [128, 2, 128]); B4 = W("Kh", [128, 2, 128])
            B5 = W("b5", [128, 2, 128]); B6 = W("b6", [128, 2, 128]); B7 = W("b7", [128, 2, 128]); B8 = W("b8", [128, 2, 128])
            nGc = W("nGc", [128, 2]); eG = W("eG", [128, 2])
            first_blk = "atcb" not in wcache
            atcb = W("atcb", [128, 2, 2, 128]); btcb = W("btcb", [128, 2, 2, 128]); rtcb = W("rtcb", [128, 2, 2, 128])
            if first_blk:
                for tz_, kz_ in ((atcb, "atcb"), (btcb, "btcb"), (rtcb, "rtcb")):
                    P.op("pool", lambda e, tz_=tz_: e.memset(tz_[:], 0.0), w=[kz_])
            bhk = W("bhk", [128, 512]); vtok = W("vtok", [128, 256]); U = W("U", [128, 4, 64]); y_sb = W("o_sb", [128, 256])
            v4 = lambda t, j: t[:, j * 512:(j + 1) * 512].rearrange("p (h t) -> p h t", t=128)
            Acur = [v4(big[0], 0), v4(big[0], 1)]; ATcur = [v4(big[1], 0), v4(big[1], 1)]
            AakT = W("attT", [128, 4, 128]); ArbT = W("ArbT", [128, 4, 128]); ArkT = W("ArkT", [128, 4, 128])
            st4 = W("oss", [128, 4]); st4b = W("st4b", [128, 4]); ysq = W("osq", [128, 256])
            if cfg.get("cut") == 1:
                return
            P.op("act", lambda e: e.activation(out=xm[0:32, 6, :], in_=xm[0:32, 6, :], func=AF.Tanh), r=["xm"], w=["xm"])
            for c in range(2):
                P.op("pe", lambda e, c=c: e.matmul(psM[:, c * 128:(c + 1) * 128], lhsT=lora_t[:, c * 128:(c + 1) * 128], rhs=xm[:, 6, :], start=True, stop=True),
                     r=["lora_t", "xm"], w=["psM"])
                P.op("pe", lambda e, c=c: e.matmul(psM[:, 256 + c * 128:256 + (c + 1) * 128], lhsT=lora_t[:, 256 + c * 128:256 + (c + 1) * 128], rhs=xm[:, 7, :], start=True, stop=True),
                     r=["lora_t", "xm"], w=["psM"])
            for c in range(2):
                P.op("act", lambda e, c=c: e.activation(out=sw[:, c, :], in_=psM[:, c * 128:(c + 1) * 128], func=AF.Sigmoid, bias=w0_t[:, c:c + 1]), r=["psM", "w0_t"], w=["qs"])
                P.op("act", lambda e, c=c: e.activation(out=av[:, c, :], in_=psM[:, 256 + c * 128:256 + (c + 1) * 128], func=AF.Sigmoid, bias=a0_t[:, c:c + 1]), r=["psM", "a0_t"], w=["fg"])
            for c in range(2):
                P.op("dve", lambda e, c=c: e.tensor_tensor_scan(out=G[:, c, :], data0=ones_t[:], data1=sw[:, c, :], initial=0.0, op0=ALU.mult, op1=ALU.add), r=["ones_t", "qs"], w=["G"])
            P.op("dve", lambda e: e.tensor_tensor(out=Gx[:], in0=G[:], in1=sw[:], op=ALU.subtract), r=["G", "qs"], w=["t1"])
            P.op("pool", lambda e: e.tensor_scalar(out=Gx[:], in0=Gx[:], scalar1=-CW, scalar2=None, op0=ALU.mult), r=["t1"], w=["t1"])
            P.op("dve", lambda e: e.tensor_scalar(out=G[:], in0=G[:], scalar1=-CW, scalar2=None, op0=ALU.mult), r=["G"], w=["G"])
            if cfg.get("cut") == 2:
                return
            for c in range(2):
                P.op("dve", lambda e, c=c: e.tensor_scalar(out=kkn[:, c, :], in0=k_[:, c, :], scalar1=kk_t[:, c:c + 1], scalar2=None, op0=ALU.mult), r=["xm", "kk_t"], w=["lf"])
            P.op("pool", lambda e: e.tensor_tensor(out=tmp[:], in0=kkn[:], in1=kkn[:], op=ALU.mult), r=["lf"], w=["Qin"])
            P.op("pe", lambda e: e.matmul(psS[0][:, 0:256], lhsT=blk1[:], rhs=tmp[:].rearrange("p c t -> p (c t)"), start=True, stop=True), r=["blk1", "Qin"], w=["psS0"])
            P.op("dve", lambda e: e.tensor_scalar(out=tmp[:].rearrange("p c t -> p (c t)"), in0=psS[0][:, 0:256], scalar1=1e-24, scalar2=None, op0=ALU.max), r=["psS0"], w=["Qin"])
            P.op("act", lambda e: e.activation(out=tmp[:], in_=tmp[:], func=AF.Ln), r=["Qin"], w=["Qin"])
            P.op("act", lambda e: e.activation(out=tmp[:], in_=tmp[:], func=AF.Exp, scale=-0.5), r=["Qin"], w=["Qin"])
            P.op("dve", lambda e: e.tensor_tensor(out=kkn[:], in0=kkn[:], in1=tmp[:], op=ALU.mult), r=["lf", "Qin"], w=["lf"])
            for c in range(2):
                P.op("dve", lambda e, c=c: e.tensor_scalar(out=k2[:, c, :], in0=av[:, c, :], scalar1=ka_t[:, c:c + 1], scalar2=omka_t[:, c:c + 1], op0=ALU.mult, op1=ALU.add),
                     r=["fg", "ka_t", "omka_t"], w=["kk"])
            P.op("pool", lambda e: e.tensor_tensor(out=k2[:], in0=k2[:], in1=k_, op=ALU.mult), r=["kk", "xm"], w=["kk"])
            P.op("pool", lambda e: e.tensor_tensor(out=bvec[:], in0=kkn[:], in1=av[:], op=ALU.mult), r=["lf", "fg"], w=["t2"])
            if cfg.get("cut") == 3:
                return
            P.op("dve", lambda e: e.tensor_scalar(out=nGc[:], in0=G[:, :, 63], scalar1=-1.0, scalar2=None, op0=ALU.mult), r=["G"], w=["nGc"])
            P.op("act", lambda e: e.activation(out=eG[:], in_=G[:, :, 127], func=AF.Exp), r=["G"], w=["eG"])
            Ep, Epx, En, Es, Esx, Ee = B1, B2, B3, B4, B5, B6
            for c in range(2):
                P.op("act", lambda e, c=c: e.activation(out=Ep[:, c, :], in_=G[:, c, :], func=AF.Exp, bias=nGc[:, c:c + 1]), r=["G", "nGc"], w=["Qoff"])
                P.op("act", lambda e, c=c: e.activation(out=Epx[:, c, :], in_=Gx[:, c, :], func=AF.Exp, bias=nGc[:, c:c + 1]), r=["t1", "nGc"], w=["Qd"])
                P.op("act", lambda e, c=c: e.activation(out=En[:, c, :], in_=G[:, c, :], func=AF.Exp, scale=-1.0, bias=G[:, c, 63:64]), r=["G"], w=["Kd"])
                P.op("act", lambda e, c=c: e.activation(out=Ee[:, c, :], in_=G[:, c, :], func=AF.Exp, scale=-1.0, bias=G[:, c, 127:128]), r=["G"], w=["b6"])
            P.op("act", lambda e: e.activation(out=Es[:], in_=G[:], func=AF.Exp), r=["G"], w=["Kh"])
            P.op("act", lambda e: e.activation(out=Esx[:], in_=Gx[:], func=AF.Exp), r=["t1"], w=["b5"])
            rt_c, at_c, kt_c, rt_s, at_s, kh, bt_c, bh = B1, B2, B3, B4, B5, B6, B7, B8
            P.op("dve", lambda e: e.tensor_tensor(out=bt_c[:], in0=bvec[:], in1=En[:], op=ALU.mult), r=["t2", "Kd"], w=["b7"])
            P.op("pool", lambda e: e.tensor_tensor(out=kt_c[:], in0=k2[:], in1=En[:], op=ALU.mult), r=["kk", "Kd", "b7"], w=["Kd"])
            for hl in range(2):
                pb = 64 * hl
                P.op("dve", lambda e, pb=pb, hl=hl: e.tensor_tensor(out=rtcb[pb:pb + 64, :, hl, :], in0=xm[pb:pb + 64, 0:2, :], in1=Ep[pb:pb + 64, :, :], op=ALU.mult), r=["xm", "Qoff"], w=["rtcb"])
            P.op("dve", lambda e: e.scalar_tensor_tensor(out=at_c[:], in0=kkn[:], scalar=-1.0, in1=Epx[:], op0=ALU.mult, op1=ALU.mult), r=["lf", "Qd"], w=["Qd"])
            P.op("dve", lambda e: e.tensor_tensor(out=rt_s[:], in0=r_, in1=Es[:], op=ALU.mult), r=["xm", "Kh"], w=["Kh"])
            for hl in range(2):
                pb = 64 * hl
                P.op("act", lambda e, pb=pb, hl=hl: e.copy(out=atcb[pb:pb + 64, :, hl, :], in_=at_c[pb:pb + 64, :, :]), r=["Qd"], w=["atcb"])
                P.op("dve", lambda e, pb=pb, hl=hl: e.tensor_copy(out=btcb[pb:pb + 64, :, hl, :], in_=bt_c[pb:pb + 64, :, :]), r=["b7"], w=["btcb"])
            P.op("dve", lambda e: e.scalar_tensor_tensor(out=at_s[:], in0=kkn[:], scalar=-1.0, in1=Esx[:], op0=ALU.mult, op1=ALU.mult), r=["lf", "b5"], w=["b5"])
            P.op("dve", lambda e: e.tensor_tensor(out=bh[:], in0=bvec[:], in1=Ee[:], op=ALU.mult), r=["t2", "b6"], w=["b8"])
            P.op("pool", lambda e: e.tensor_tensor(out=kh[:], in0=k2[:], in1=Ee[:], op=ALU.mult), r=["kk", "b6", "b8"], w=["b6"])
            if cfg.get("cut") == 4:
                return
            for c in range(2):
                P.op("pe", lambda e, c=c: e.transpose(out=psM[:, c * 128:(c + 1) * 128], in_=bh[:, c, :], identity=ident[:]), r=["b8", "ident"], w=["psM"])
                P.op("pe", lambda e, c=c: e.transpose(out=psM[:, 256 + c * 128:256 + (c + 1) * 128], in_=kh[:, c, :], identity=ident[:]), r=["b6", "ident"], w=["psM"])
            P.op("act", lambda e: e.copy(out=bhk[:], in_=psM[:]), r=["psM"], w=["bhk"])
            for c in range(2):
                P.op("pe", lambda e, c=c: e.transpose(out=psS[1][:, c * 128:(c + 1) * 128], in_=v_[:, c, :], identity=ident[:]), r=["xm", "ident"], w=["psS1"])
            P.op("act", lambda e: e.copy(out=vtok[:], in_=psS[1][:, 0:256]), r=["psS1"], w=["vtok"])
            if cfg.get("cut") == 5:
                return
            def pair(lhs, lk, rhs, rk, dst, dk, msk, mk):
                pp, pk = next_psP()
                p4 = pp[:].rearrange("p (h t) -> p h t", t=128)
                for c in range(2):
                    P.op("pe", lambda e, c=c: e.matmul(pp[:, c * 256:(c + 1) * 256], lhsT=lhs[:, c, :], rhs=rhs[:, c, :, :].rearrange("p a t -> p (a t)"), start=True, stop=True), r=[lk, rk], w=[pk])
                dv = dst if dk.startswith("big") else dst[:]
                P.op("dve", lambda e: e.tensor_tensor(out=dv, in0=p4, in1=bc4(msk), op=ALU.mult), r=[pk, mk], w=[dk])
            pair(at_c, "Qd", btcb, "btcb", Acur[0], "big0a", m_gt, "m_gt")
            if cfg.get("cut") == 51:
                return
            pair(bt_c, "b7", atcb, "atcb", ATcur[0], "big1a", m_lt, "m_lt")
            if cfg.get("cut") == 52:
                return
            pair(kt_c, "Kd", atcb, "atcb", AakT, "attT", m_lt, "m_lt")
            if cfg.get("cut") == 53:
                return
            if cfg.get("cut") == 6:
                return
            o4 = psO[:, 0:256].rearrange("p (h v) -> p h v", v=64)
            for c in range(2):
                P.op("pe", lambda e, c=c: e.matmul(psO[:, c * 128:(c + 1) * 128], lhsT=at_s[:, c, :], rhs=Z_B[:, c, :, :].rearrange("p a v -> p (a v)"), start=True, stop=False), r=["b5", "Z_B"], w=["psO"])
                for hl in range(2):
                    h = 2 * c + hl
                    P.op("pe", lambda e, h=h, hl=hl: e.matmul(o4[:, h, :], lhsT=AakT[:, h, :], rhs=vtok[:, h * 64:(h + 1) * 64], start=False, stop=(hl == 1)), r=["attT", "vtok"], w=["psO"])
            P.op("act", lambda e: e.copy(out=U[:], in_=o4), r=["psO"], w=["U"])
            A0 = Acur[0]; AT0 = ATcur[0]; Tm = Acur[1]; TTm = ATcur[1]; X1 = ArbT[:]; X2 = ArkT[:]
            bcm = lambda j: hmask[:, j, :].unsqueeze(1).to_broadcast([128, 4, 128])
            idb = ident[:].unsqueeze(1).to_broadcast([128, 4, 128])

            inv_banks = [(psP[0], "psP0"), (psP[1], "psP1"), (psP[2], "psP2"), (psS[0], "psS0"), (psS[1], "psS1")]

            def mm(lhsT, lk, rhs, rk):
                pp, pk = inv_banks[inv_i[0] % 5]
                inv_i[0] += 1
                p4 = pp[:].rearrange("p (h t) -> p h t", t=128)
                for h in range(4):
                    P.op("pe", lambda e, h=h: e.matmul(p4[:, h, :], lhsT=lhsT[:, h, :], rhs=rhs[:, h, :], start=True, stop=True), r=[lk, rk], w=[pk])
                return p4, pk
            cpy = lambda dst, dk, p4, pk: P.op("act", lambda e: e.copy(out=dst, in_=p4), r=[pk], w=[dk])
            acc = lambda dst, dk, p4, pk: P.op("dve", lambda e: e.tensor_tensor(out=dst, in0=dst, in1=p4, op=ALU.add), r=[dk, pk], w=[dk])
            P.op("pool", lambda e: e.tensor_tensor(out=X1, in0=A0, in1=bcm(0), op=ALU.mult), r=["big0a", "hmask"], w=["ArbT"])
            P.op("dve", lambda e: e.tensor_tensor(out=X2, in0=AT0, in1=bcm(0), op=ALU.mult), r=["big1a", "hmask"], w=["ArkT"])
            q2, q2k = mm(X2, "ArkT", X1, "ArbT"); q2t, q2tk = mm(X1, "ArbT", X2, "ArkT")
            P.op("dve", lambda e: e.tensor_tensor(out=Tm, in0=X1, in1=idb, op=ALU.add), r=["ArbT", "ident"], w=["big0b"])
            P.op("dve", lambda e: e.tensor_tensor(out=TTm, in0=X2, in1=idb, op=ALU.add), r=["ArkT", "ident"], w=["big1b"])
            cpy(X1, "ArbT", q2, q2k); cpy(X2, "ArkT", q2t, q2tk)
            for rep_ in range(2):
                n_, nk = mm(TTm, "big1b", X1, "ArbT"); m_, mk_ = mm(X1, "ArbT", TTm, "big1b")
                acc(Tm, "big0b", n_, nk); acc(TTm, "big1b", m_, mk_)
                if rep_ == 0:
                    q4, q4k = mm(X2, "ArkT", X1, "ArbT"); q4t, q4tk = mm(X1, "ArbT", X2, "ArkT")
                    cpy(X1, "ArbT", q4, q4k); cpy(X2, "ArkT", q4t, q4tk)
            for lv in range(1, 5):
                P.op("pool", lambda e, lv=lv: e.tensor_tensor(out=X1, in0=A0, in1=bcm(lv), op=ALU.mult), r=["big0a", "hmask"], w=["ArbT"])
                P.op("dve", lambda e, lv=lv: e.tensor_tensor(out=X2, in0=AT0, in1=bcm(lv), op=ALU.mult), r=["big1a", "hmask"], w=["ArkT"])
                n1, n1k = mm(X2, "ArkT", Tm, "big0b"); m1, m1k = mm(X1, "ArbT", TTm, "big1b")
                cpy(X1, "ArbT", n1, n1k); cpy(X2, "ArkT", m1, m1k)
                if lv < 4:
                    n2, n2k = mm(TTm, "big1b", X1, "ArbT")
                m2, m2k = mm(Tm, "big0b", X2, "ArkT")
                if lv < 4:
                    acc(Tm, "big0b", n2, n2k)
                acc(TTm, "big1b", m2, m2k)
            for h in range(4):
                P.op("pe", lambda e, h=h: e.matmul(o4[:, h, :], lhsT=TTm[:, h, :], rhs=U[:, h, :], start=True, stop=True), r=["big1b", "U"], w=["psO"])
            P.op("act", lambda e: e.copy(out=U[:], in_=o4), r=["psO"], w=["U"])
            pair(bt_c, "b7", rtcb, "rtcb", ArbT, "ArbT", m_le, "m_le")
            pair(kt_c, "Kd", rtcb, "rtcb", ArkT, "ArkT", m_le, "m_le")
            for c in range(2):
                P.op("pe", lambda e, c=c: e.matmul(psO[:, c * 128:(c + 1) * 128], lhsT=rt_s[:, c, :], rhs=Z_B[:, c, :, :].rearrange("p a v -> p (a v)"), start=True, stop=False), r=["Kh", "Z_B"], w=["psO"])
                for hl in range(2):
                    h = 2 * c + hl
                    P.op("pe", lambda e, h=h: e.matmul(o4[:, h, :], lhsT=ArbT[:, h, :], rhs=U[:, h, :], start=False, stop=False), r=["ArbT", "U"], w=["psO"])
                    P.op("pe", lambda e, h=h, hl=hl: e.matmul(o4[:, h, :], lhsT=ArkT[:, h, :], rhs=vtok[:, h * 64:(h + 1) * 64], start=False, stop=(hl == 1)), r=["ArkT", "vtok"], w=["psO"])
            P.op("act", lambda e: e.copy(out=y_sb[:], in_=psO[:, 0:256]), r=["psO"], w=["o_sb"])
            if cfg.get("cut") == 8:
                return
            Uf = U[:].rearrange("p h v -> p (h v)")
            for c in range(2):
                P.op("pe", lambda e, c=c: e.matmul(psS[1][:, c * 128:(c + 1) * 128], lhsT=bhk[:, c * 128:(c + 1) * 128], rhs=Uf[:, c * 128:(c + 1) * 128], start=True, stop=False),
                     r=["bhk", "U"], w=["psS1"])
                P.op("pe", lambda e, c=c: e.matmul(psS[1][:, c * 128:(c + 1) * 128], lhsT=bhk[:, 256 + c * 128:256 + (c + 1) * 128], rhs=vtok[:, c * 128:(c + 1) * 128], start=False, stop=True),
                     r=["bhk", "vtok"], w=["psS1"])
            for c in range(2):
                for hl in range(2):
                    pb = 64 * hl
                    P.op("dve", lambda e, c=c, pb=pb, hl=hl: e.scalar_tensor_tensor(out=Z_B[pb:pb + 64, c, hl, :], in0=Z_B[pb:pb + 64, c, hl, :], scalar=eG[pb:pb + 64, c:c + 1],
                                                                                  in1=psS[1][pb:pb + 64, c * 128 + hl * 64:c * 128 + (hl + 1) * 64], op0=ALU.mult, op1=ALU.add),
                         r=["Z_B", "eG", "psS1"], w=["Z_B"])
            if cfg.get("cut") == 9:
                return
            P.op("pool", lambda e: e.tensor_tensor(out=tmp[:], in0=r_, in1=k2[:], op=ALU.mult), r=["xm", "kk"], w=["Qin"])
            for c in range(2):
                P.op("dve", lambda e, c=c: e.tensor_scalar(out=tmp[:, c, :], in0=tmp[:, c, :], scalar1=rk_t[:, c:c + 1], scalar2=None, op0=ALU.mult), r=["Qin", "rk_t"], w=["Qin"])
            for c in range(2):
                P.op("pe", lambda e, c=c: e.matmul(psM[:, 2 * c:2 * c + 2], lhsT=tmp[:, c, :], rhs=hsel[:], start=True, stop=True), r=["Qin", "hsel"], w=["psM"])
            P.op("act", lambda e: e.copy(out=st4b[:], in_=psM[:, 0:4]), r=["psM"], w=["st4b"])
            if cfg.get("cut") == 10:
                return
            y3 = y_sb[:].rearrange("p (h v) -> p h v", v=64)
            P.op("dve", lambda e: e.tensor_reduce(out=st4[:], in_=y3, axis=AX.X, op=ALU.add), r=["o_sb"], w=["oss"])
            P.op("dve", lambda e: e.tensor_scalar(out=st4[:], in0=st4[:], scalar1=-1.0 / HD, scalar2=None, op0=ALU.mult), r=["oss"], w=["oss"])
            P.op("dve", lambda e: e.tensor_tensor(out=y3, in0=y3, in1=st4[:].unsqueeze(2).to_broadcast([128, 4, 64]), op=ALU.add), r=["o_sb", "oss"], w=["o_sb"])
            P.op("pool", lambda e: e.tensor_tensor(out=ysq[:], in0=y_sb[:], in1=y_sb[:], op=ALU.mult), r=["o_sb"], w=["osq"])
            P.op("dve", lambda e: e.tensor_reduce(out=st4[:], in_=ysq[:].rearrange("p (h v) -> p h v", v=64), axis=AX.X, op=ALU.add), r=["osq"], w=["oss"])
            P.op("act", lambda e: e.activation(out=st4[:], in_=st4[:], func=AF.Ln, scale=1.0 / HD, bias=GN_EPS), r=["oss"], w=["oss"])
            P.op("act", lambda e: e.activation(out=st4[:], in_=st4[:], func=AF.Exp, scale=-0.5), r=["oss"], w=["oss"])
            P.op("dve", lambda e: e.tensor_tensor(out=y3, in0=y3, in1=st4[:].unsqueeze(2).to_broadcast([128, 4, 64]), op=ALU.mult), r=["o_sb", "oss"], w=["o_sb"])
            P.op("pool", lambda e: e.tensor_tensor(out=y_sb[:], in0=y_sb[:], in1=gnw_bc[:], op=ALU.mult), r=["o_sb", "gnw_bc"], w=["o_sb"])
            P.op("dve", lambda e: e.tensor_tensor(out=y_sb[:], in0=y_sb[:], in1=gnb_bc[:], op=ALU.add), r=["o_sb", "gnb_bc"], w=["o_sb"])
            P.op("pool", lambda e: e.tensor_tensor(out=ysq[:].rearrange("p (h v) -> p h v", v=64), in0=vtok[:].rearrange("p (h v) -> p h v", v=64),
                                                    in1=st4b[:].unsqueeze(2).to_broadcast([128, 4, 64]), op=ALU.mult), r=["vtok", "st4b"], w=["osq"])
            P.op("dve", lambda e: e.tensor_tensor(out=y_sb[:], in0=y_sb[:], in1=ysq[:], op=ALU.add), r=["o_sb", "osq"], w=["o_sb"])
            P.op("dve", lambda e: e.tensor_tensor(out=mix[:, 256:512], in0=y_sb[:], in1=sg[:, 256:512], op=ALU.mult), r=["o_sb", "sg"], w=["mixB"])
            if i == NT - 1:
                for c in range(2):
                    P.op("pe", lambda e, c=c: e.transpose(out=psM[:, c * 128:(c + 1) * 128], in_=Z_B[:, c, :, :].rearrange("p a v -> p (a v)"), identity=ident[:]), r=["Z_B", "ident"], w=["psM"])
                P.op("act", lambda e: e.copy(out=bhk[:, 0:256], in_=psM[:, 0:256]), r=["psM"], w=["bhk"])
                for c in range(2):
                    for hl in range(2):
                        store(oBw_p[l, 2 * c + hl], bhk[64 * hl:64 * hl + 64, c * 128 + hl * 64:c * 128 + (hl + 1) * 64], "oBw", r=["bhk"])
                for ch in range(8):
                    n = 128 if ch < 6 else 32
                    o0 = ch * 128 if ch < 6 else 768 + (ch - 6) * 32
                    store(oBs_p[l, o0:o0 + n].rearrange("(p o) -> p o", o=1), XS[0:n, ch, 128:129], "oBs", r=["XS"])

        def tile_back(l, i, X, xk):
            if dbg and l == 0:
                store(mix_dbg[i * 128:(i + 1) * 128, :], mix[:], "mixdbg", r=["mixA", "mixB", "mixC", "mixD"])
            for kc in range(8):
                P.op("pe", lambda e, kc=kc: e.transpose(out=psT[:, kc, :], in_=mix[:, kc * 128:(kc + 1) * 128], identity=identb[:]), r=["mixA", "mixB", "mixC", "mixD", "identb"], w=["psT"])
            P.op("act", lambda e: e.copy(out=mixT[:], in_=psT[:]), r=["psT"], w=["hT"])
            for nb in range(2):
                pp, pk = next_psP()
                for kc in range(8):
                    P.op("pe", lambda e, kc=kc, pp=pp, nb=nb: e.matmul(pp[:], lhsT=mixT[:, kc, :], rhs=wout_b[:, kc, nb * 512:(nb + 1) * 512], start=(kc == 0), stop=(kc == 7)),
                         r=["hT", "wout_b"], w=[pk])
                P.op("dve", lambda e, pp=pp, nb=nb: e.tensor_tensor(out=xo[:, nb * 512:(nb + 1) * 512], in0=pp[:], in1=gate_bc[:, nb * 512:(nb + 1) * 512], op=ALU.mult),
                     r=[pk, "gate_bc"], w=["h1"])
            P.op("dve", lambda e: e.tensor_tensor(out=xo[:], in0=xo[:], in1=X[:], op=ALU.add), r=["h1", xk], w=["h1"])
            P.dma(lambda e: e.dma_start(out=y_p[i * 128:(i + 1) * 128, :], in_=xo[:]), "xo_out", r=["h1"], w=[f"y{i}"])

        P.op("pool", lambda e: e.memset(mix[:], 0.0), w=["mixA", "mixB", "mixC", "mixD"])

        def sample_phase(l0):
            N = NS
            sl = slice(0, N)
            bq = lambda ap, shape: ap.to_broadcast(shape)
            xs_t = iu[sl].rearrange("p a b -> p (a b)")
            projF = kT_hist[sl].rearrange("p a b -> p (a b)").bitcast(F32)
            projT = v_hist[sl].rearrange("p a b c -> p (a b c)").bitcast(F32)
            scores = band[sl].rearrange("p a b -> p (a b)")
            sc4 = scores.rearrange("p (c n h) -> p c n h", c=3, h=4)
            xmv = xm[sl].rearrange("p a b -> p (a b)")
            xsv = XS[sl].rearrange("p a b -> p (a b)")
            def Wv(nm):
                t_ = W(nm, [128, 2, 128])
                v_ = t_[sl]
                return v_ if len(v_.shape) == 2 else v_.rearrange("p a b -> p (a b)")
            def op(eng, meth, r, w, **kw):
                P.op(eng, lambda e: getattr(e, meth)(**kw), r=r, w=w)
            def tt(eng, out, a, b, o, r, w):
                op(eng, "tensor_tensor", r, w, out=out, in0=a, in1=b, op=o)
            def rowload(dst_ap, src_row, key, wkey):
                load(dst_ap, src_row.partition_broadcast(N), key, [wkey])
            P.dma(lambda e: e.dma_start(out=xs_t, in_=(x_s if l0 == 0 else y_s)), "xs_in", r=["ys_d"], w=["iu0", "iu1"])
            load(scsT, c_sT, "scs", ["qT"])
            op("act", "activation", ["qT"], ["qT"], out=scsT, in_=scsT, func=AF.Silu)
            ltw = adab_pc[:, 0:16]; lta = adab_pc[:, 16:32]; pTd = adab_pc[:, 32:48]
            cs_r = cos_t; sn_r = sin_t
            rowload(cs_r[sl], k_cs[0:1, :], "csr", "cos_t"); rowload(sn_r[sl], k_sn[0:1, :], "snr", "sin_t")
            for l in (l0,):
                layer_setup(l, sample=True)
                P.op("pool", lambda e: e.memset(adab_pc[:, 0:48], 0.0), r=["adab_pc"], w=["lt", "pTd", "adab_pc"])
                op("act", "activation", ["iu0", "iu1"], ["h1", "ssq"], out=h1[sl], in_=xs_t, func=AF.Square, accum_out=ssq[sl])
                op("act", "activation", ["ssq"], ["rstd"], out=rstd[sl], in_=ssq[sl], func=AF.Sqrt, scale=1.0 / D, bias=EPS)
                op("dve", "reciprocal", ["rstd"], ["rstd"], out=rstd[sl], in_=rstd[sl])
                op("dve", "scalar_tensor_tensor", ["iu0", "iu1", "rstd", "gs_bc"], ["h1"], out=h1[sl], in0=xs_t, scalar=rstd[sl, 0:1], in1=gs_bc[sl], op0=ALU.mult, op1=ALU.mult)
                tt("pool", hb[sl], h1[sl], shift_bc[sl], ALU.add, ["h1", "shift_bc"], ["hb"])
                for kc in range(8):
                    P.op("pe", lambda e, kc=kc: e.transpose(out=psT[:, kc, 0:N], in_=hb[sl, kc * 128:(kc + 1) * 128], identity=identb[sl, sl]), r=["hb", "identb"], w=["psT"])
                op("act", "copy", ["psT"], ["hT"], out=hT[:, :, 0:N], in_=psT[:, :, 0:N])

                def proj(c0, n):
                    pp, pk = next_psP()
                    for kc in range(8):
                        P.op("pe", lambda e, kc=kc: e.matmul(pp[sl, 0:n], lhsT=hT[:, kc, 0:N], rhs=win_b[:, kc, c0:c0 + n], start=(kc == 0), stop=(kc == 7)), r=["win_b", "hT"], w=[pk])
                    return pp, pk
                for c0, n in ((0, 512), (512, 512), (1024, 320)):
                    pp, pk = proj(c0, n)
                    op("act", "copy", [pk], ["kT_hist"], out=projF[:, c0:c0 + n], in_=pp[sl, 0:n])
                for c0, n in ((0, 512), (512, 512), (1024, 256)):
                    pp, pk = proj(NFM + c0, n)
                    op("act", "copy", [pk], ["v_hist"], out=projT[:, c0:c0 + n], in_=pp[sl, 0:n])
                for gb in range(2):
                    pp, pk = proj(NFM + 1280 + gb * 512, 512)
                    op("act", "activation", [pk], ["sg"], out=sg[sl, gb * 512:(gb + 1) * 512], in_=pp[sl, 0:512], func=AF.Silu)
                store(oBs_s[l], projF[:, 512:1344], "oBs_s", r=["kT_hist"])
                qA = Wv("qs"); fA = Wv("fg"); kA = Wv("kk"); lbr = Wv("lf"); oA = Wv("o_sb"); po = Wv("osq")
                lg = xt[0][sl].rearrange("p (l f) -> p l f", l=4)
                rowload(xt[0][sl], lbrow.rearrange("l f -> (l f)").unsqueeze(0), "lbr", "xt0")
                op("act", "activation", ["xt0"], ["xt0"], out=xt[0][sl], in_=xt[0][sl], func=AF.Exp)
                op("dve", "tensor_reduce", ["xt0"], ["t1"], out=Wv("t1"), in_=lg.rearrange("p l f -> p f l"), axis=AX.X, op=ALU.add)
                op("dve", "reciprocal", ["t1"], ["t1"], out=Wv("t1"), in_=Wv("t1"))
                op("pool", "memset", [], ["lf"], ap=lbr, constant=0.0)
                for l2 in range(1, l + 1):
                    tt("dve", lbr, lbr, lg[:, l2, :], ALU.add, ["lf", "xt0"], ["lf"])
                tt("dve", lbr, lbr, Wv("t1"), ALU.mult, ["lf", "t1"], ["lf"])
                op("act", "activation", ["kT_hist"], ["qs"], out=qA, in_=projF[:, 0:256], func=AF.Silu)
                op("act", "activation", ["kT_hist"], ["fg"], out=fA, in_=projF[:, 256:512], func=AF.Sigmoid)
                op("dve", "tensor_scalar", ["fg"], ["kk"], out=kA, in0=fA, scalar1=-1.0, scalar2=1.0, op0=ALU.mult, op1=ALU.add)
                tt("dve", Wv("t2"), kA, lbr, ALU.mult, ["kk", "lf"], ["t2"])
                tt("dve", fA, fA, Wv("t2"), ALU.add, ["fg", "t2"], ["fg"])
                op("dve", "tensor_scalar", ["fg"], ["kk"], out=kA, in0=fA, scalar1=-1.0, scalar2=1.0, op0=ALU.mult, op1=ALU.add)
                iA = projT[:, 0:256]
                S3s = [(xt[0][sl].rearrange("p (k v) -> p k v", v=64), "xt0"), (big[0][sl].rearrange("p (k v) -> p k v", v=64), "big0")]
                T3s = [(h1[sl].rearrange("p (k v) -> p k v", v=64), "h1"), (big[1][sl].rearrange("p (k v) -> p k v", v=64), "big1")]
                for h in range(4):
                    for pc in range(4):
                        k0 = h * 64 + pc * 16
                        S3, sk_ = S3s[pc % 2]; T3, tk_ = T3s[pc % 2]
                        load(S3, sA_in[l, :, h, pc * 16:(pc + 1) * 16, :], "sA_ld" + sk_, [sk_])
                        tt("dve", S3, S3, bq(fA[:, k0:k0 + 16].unsqueeze(2), [N, 16, 64]), ALU.mult, [sk_, "fg"], [sk_])
                        tt("pool", T3, bq(kA[:, k0:k0 + 16].unsqueeze(2), [N, 16, 64]), bq(iA[:, h * 64:(h + 1) * 64].unsqueeze(1), [N, 16, 64]), ALU.mult, ["kk", "v_hist"], [tk_])
                        tt("dve", S3, S3, T3, ALU.add, [sk_, tk_], [sk_])
                        store(oA_s[l, :, h, pc * 16:(pc + 1) * 16, :], S3, "oA_s" + sk_, r=[sk_])
                        tt("pool", T3, S3, bq(qA[:, k0:k0 + 16].unsqueeze(2), [N, 16, 64]), ALU.mult, [sk_, "qs"], [tk_])
                        dsto = oA[:, h * 64:(h + 1) * 64] if pc == 0 else po[:, 0:64]
                        op("dve", "tensor_reduce", [tk_], ["o_sb" if pc == 0 else "osq"], out=dsto, in_=T3.rearrange("p k v -> p v k"), axis=AX.X, op=ALU.add)
                        if pc:
                            tt("dve", oA[:, h * 64:(h + 1) * 64], oA[:, h * 64:(h + 1) * 64], po[:, 0:64], ALU.add, ["o_sb", "osq"], ["o_sb"])
                st4 = W("oss", [128, 4])[sl]
                tt("dve", po, oA, oA, ALU.mult, ["o_sb"], ["osq"])
                op("dve", "tensor_reduce", ["osq"], ["oss"], out=st4, in_=po.rearrange("p (h v) -> p h v", v=64), axis=AX.X, op=ALU.add)
                op("act", "activation", ["oss"], ["oss"], out=st4, in_=st4, func=AF.Sqrt, scale=1.0 / HD, bias=EPS)
                op("dve", "reciprocal", ["oss"], ["oss"], out=st4, in_=st4)
                tt("dve", oA.rearrange("p (h v) -> p h v", v=64), oA.rearrange("p (h v) -> p h v", v=64), bq(st4.unsqueeze(2), [N, 4, 64]), ALU.mult, ["o_sb", "oss"], ["o_sb"])
                tt("pool", oA, oA, onorm_bc[sl], ALU.mult, ["o_sb", "onorm_bc"], ["o_sb"])
                tt("dve", mix[sl, 0:256], oA, sg[sl, 0:256], ALU.mult, ["o_sb", "sg"], ["mixA"])
                load(xmv[:, 0:BSW], sBs_in[l], "sBs_ld", ["xm"])
                rowload(xsv[:, 0:BSW], mu_row[l:l + 1, :], "mur", "XS")
                xs_ = projF[:, 512:1344]
                tt("dve", xmv[:, 0:BSW], xmv[:, 0:BSW], xs_, ALU.subtract, ["xm", "kT_hist"], ["xm"])
                tt("dve", xmv[:, 0:BSW], xmv[:, 0:BSW], xsv[:, 0:BSW], ALU.mult, ["xm", "XS"], ["xm"])
                tt("dve", xmv[:, 0:BSW], xmv[:, 0:BSW], xs_, ALU.add, ["xm", "kT_hist"], ["xm"])
                rB = xmv[:, 0:256]; kB = xmv[:, 256:512]; vB = xmv[:, 512:768]
                op("act", "activation", ["xm"], ["xm"], out=xmv[:, 768:800], in_=xmv[:, 768:800], func=AF.Tanh)
                P.op("pe", lambda e: e.transpose(out=psM[0:32, 0:N], in_=xmv[:, 768:800], identity=ident[sl, sl]), r=["xm", "ident"], w=["psM"])
                P.op("pe", lambda e: e.transpose(out=psM[0:32, 16:16 + N], in_=xmv[:, 800:832], identity=ident[sl, sl]), r=["xm", "ident"], w=["psM"])
                op("act", "copy", ["psM"], ["lt"], out=ltw[0:32, 0:N], in_=psM[0:32, 0:N])
                op("act", "copy", ["psM"], ["lt"], out=lta[0:32, 0:N], in_=psM[0:32, 16:16 + N])
                P.op("pe", lambda e: e.matmul(psS[0][sl, 0:256], lhsT=ltw[:, 0:N], rhs=lora_t[:, 0:256], start=True, stop=True), r=["lt", "lora_t"], w=["psS0"])
                P.op("pe", lambda e: e.matmul(psS[0][sl, 256:512], lhsT=lta[:, 0:N], rhs=lora_t[:, 256:512], start=True, stop=True), r=["lt", "lora_t"], w=["psS0"])
                wB = Wv("G"); aB = Wv("t1"); prm = Wv("Qin"); kkn = Wv("lf"); k2 = Wv("kk"); bv = Wv("t2"); av_ = Wv("Qoff"); yB = Wv("Qd"); tq = Wv("Kd")
                rowload(prm, rowp["b_w0"][l:l + 1, :], "rw0", "Qin")
                tt("dve", wB, psS[0][sl, 0:256], prm, ALU.add, ["psS0", "Qin"], ["G"])
                op("act", "activation", ["G"], ["G"], out=wB, in_=wB, func=AF.Sigmoid)
                op("act", "activation", ["G"], ["G"], out=wB, in_=wB, func=AF.Exp, scale=-CW)
                rowload(prm, rowp["b_a0"][l:l + 1, :], "ra0", "Qin")
                tt("dve", aB, psS[0][sl, 256:512], prm, ALU.add, ["psS0", "Qin"], ["t1"])
                op("act", "activation", ["t1"], ["t1"], out=aB, in_=aB, func=AF.Sigmoid)
                rowload(prm, rowp["b_k_k"][l:l + 1, :], "rkk", "Qin")
                tt("dve", kkn, kB, prm, ALU.mult, ["xm", "Qin"], ["lf"])
                tt("dve", tq, kkn, kkn, ALU.mult, ["lf"], ["Kd"])
                op("dve", "tensor_reduce", ["Kd"], ["oss"], out=st4, in_=tq.rearrange("p (h v) -> p h v", v=64), axis=AX.X, op=ALU.add)
                op("act", "activation", ["oss"], ["oss"], out=st4, in_=st4, func=AF.Sqrt)
                op("dve", "tensor_scalar", ["oss"], ["oss"], out=st4, in0=st4, scalar1=1e-12, scalar2=None, op0=ALU.max)
                op("dve", "reciprocal", ["oss"], ["oss"], out=st4, in_=st4)
                tt("dve", kkn.rearrange("p (h v) -> p h v", v=64), kkn.rearrange("p (h v) -> p h v", v=64), bq(st4.unsqueeze(2), [N, 4, 64]), ALU.mult, ["lf", "oss"], ["lf"])
                rowload(prm, rowp["b_k_a"][l:l + 1, :], "rka", "Qin")
                tt("dve", k2, aB, prm, ALU.mult, ["t1", "Qin"], ["kk"])
                op("dve", "tensor_scalar", ["Qin"], ["Qin"], out=prm, in0=prm, scalar1=-1.0, scalar2=1.0, op0=ALU.mult, op1=ALU.add)
                tt("dve", k2, k2, prm, ALU.add, ["kk", "Qin"], ["kk"])
                tt("dve", k2, k2, kB, ALU.mult, ["kk", "xm"], ["kk"])
                tt("dve", bv, kkn, aB, ALU.mult, ["lf", "t1"], ["t2"])
                op("dve", "tensor_scalar", ["lf"], ["Qoff"], out=av_, in0=kkn, scalar1=-1.0, scalar2=None, op0=ALU.mult)
                rowload(prm, rowp["b_r_k"][l:l + 1, :], "rrk", "Qin")
                tt("dve", tq, rB, k2, ALU.mult, ["xm", "kk"], ["Kd"])
                tt("dve", tq, tq, prm, ALU.mult, ["Kd", "Qin"], ["Kd"])
                rk4 = W("st4b", [128, 4])[sl]
                op("dve", "tensor_reduce", ["Kd"], ["st4b"], out=rk4, in_=tq.rearrange("p (h v) -> p h v", v=64), axis=AX.X, op=ALU.add)
                sa16 = adab_pc[sl, 48:64]
                for h in range(4):
                    hs = slice(h * 64, (h + 1) * 64)
                    for pc in range(4):
                        i0 = h * 64 + pc * 16
                        S3, sk_ = S3s[pc % 2]; T3, tk_ = T3s[pc % 2]
                        load(S3, sBw_in[l, :, h, pc * 16:(pc + 1) * 16, :], "sB_ld" + sk_, [sk_])
                        tt("pool", T3, S3, bq(av_[:, hs].unsqueeze(1), [N, 16, 64]), ALU.mult, [sk_, "Qoff"], [tk_])
                        op("dve", "tensor_reduce", [tk_], ["sa16"], out=sa16, in_=T3, axis=AX.X, op=ALU.add)
                        tt("dve", S3, S3, bq(wB[:, hs].unsqueeze(1), [N, 16, 64]), ALU.mult, [sk_, "G"], [sk_])
                        tt("pool", T3, bq(sa16.unsqueeze(2), [N, 16, 64]), bq(bv[:, hs].unsqueeze(1), [N, 16, 64]), ALU.mult, ["sa16", "t2"], [tk_])
                        tt("dve", S3, S3, T3, ALU.add, [sk_, tk_], [sk_])
                        tt("pool", T3, bq(vB[:, i0:i0 + 16].unsqueeze(2), [N, 16, 64]), bq(k2[:, hs].unsqueeze(1), [N, 16, 64]), ALU.mult, ["xm", "kk"], [tk_])
                        tt("dve", S3, S3, T3, ALU.add, [sk_, tk_], [sk_])
                        store(oBw_s[l, :, h, pc * 16:(pc + 1) * 16, :], S3, "oB_s" + sk_, r=[sk_])
                        tt("pool", T3, S3, bq(rB[:, hs].unsqueeze(1), [N, 16, 64]), ALU.mult, [sk_, "xm"], [tk_])
                        op("dve", "tensor_reduce", [tk_], ["Qd"], out=yB[:, i0:i0 + 16], in_=T3, axis=AX.X, op=ALU.add)
                y3 = yB.rearrange("p (h v) -> p h v", v=64)
                op("dve", "tensor_reduce", ["Qd"], ["oss"], out=st4, in_=y3, axis=AX.X, op=ALU.add)
                op("dve", "tensor_scalar", ["oss"], ["oss"], out=st4, in0=st4, scalar1=-1.0 / HD, scalar2=None, op0=ALU.mult)
                tt("dve", y3, y3, bq(st4.unsqueeze(2), [N, 4, 64]), ALU.add, ["Qd", "oss"], ["Qd"])
                tt("dve", tq, yB, yB, ALU.mult, ["Qd"], ["Kd"])
                op("dve", "tensor_reduce", ["Kd"], ["oss"], out=st4, in_=tq.rearrange("p (h v) -> p h v", v=64), axis=AX.X, op=ALU.add)
                op("act", "activation", ["oss"], ["oss"], out=st4, in_=st4, func=AF.Sqrt, scale=1.0 / HD, bias=GN_EPS)
                op("dve", "reciprocal", ["oss"], ["oss"], out=st4, in_=st4)
                tt("dve", y3, y3, bq(st4.unsqueeze(2), [N, 4, 64]), ALU.mult, ["Qd", "oss"], ["Qd"])
                tt("dve", yB, yB, gnw_bc[sl], ALU.mult, ["Qd", "gnw_bc"], ["Qd"])
                tt("dve", yB, yB, gnb_bc[sl], ALU.add, ["Qd", "gnb_bc"], ["Qd"])
                tt("dve", tq.rearrange("p (h v) -> p h v", v=64), vB.rearrange("p (h v) -> p h v", v=64), bq(rk4.unsqueeze(2), [N, 4, 64]), ALU.mult, ["xm", "st4b"], ["Kd"])
                tt("dve", yB, yB, tq, ALU.add, ["Qd", "Kd"], ["Qd"])
                tt("dve", mix[sl, 256:512], yB, sg[sl, 256:512], ALU.mult, ["Qd", "sg"], ["mixB"])
                qk = projT[:, 512:1024]; qkn_ = big[0][sl, 0:512]; qkr_ = Wv2 = None
                qkr_ = W("bhk", [128, 512])[sl]
                op("act", "activation", ["v_hist"], ["big0"], out=qkn_, in_=qk, func=AF.Square)
                ss8 = qk_ss[sl]
                op("dve", "tensor_reduce", ["big0"], ["qk_ss"], out=ss8, in_=qkn_.rearrange("p (h d) -> p h d", d=64), axis=AX.X, op=ALU.add)
                op("act", "activation", ["qk_ss"], ["qk_ss"], out=ss8, in_=ss8, func=AF.Sqrt, scale=1.0 / HD, bias=EPS)
                op("dve", "reciprocal", ["qk_ss"], ["qk_ss"], out=ss8, in_=ss8)
                tt("dve", qkn_.rearrange("p (h d) -> p h d", d=64), qk.rearrange("p (h d) -> p h d", d=64), bq(ss8.unsqueeze(2), [N, 8, 64]), ALU.mult, ["v_hist", "qk_ss"], ["big0"])
                tt("dve", qkn_.rearrange("p (a h d) -> p a h d", a=2, d=64), qkn_.rearrange("p (a h d) -> p a h d", a=2, d=64),
                   bq(qkg_bc[sl].rearrange("p (a d) -> p a d", d=64).unsqueeze(2), [N, 2, 4, 64]), ALU.mult, ["big0", "qkg_bc"], ["big0"])
                q3 = qkn_.rearrange("p (h d) -> p h d", d=64); o3 = qkr_.rearrange("p (h d) -> p h d", d=64)
                ra = W("b5", [128, 2, 128])[sl].rearrange("p c (a b) -> p (c a) b", b=32); rb = W("b6", [128, 2, 128])[sl].rearrange("p c (a b) -> p (c a) b", b=32)
                csb = bq(cs_r[sl].unsqueeze(1), [N, 8, 32]); snb = bq(sn_r[sl].unsqueeze(1), [N, 8, 32])
                tt("dve", ra, q3[:, :, 0:32], csb, ALU.mult, ["big0", "cos_t"], ["b5"])
                tt("dve", rb, q3[:, :, 32:64], snb, ALU.mult, ["big0", "sin_t"], ["b6"])
                tt("dve", o3[:, :, 0:32], ra, rb, ALU.subtract, ["b5", "b6"], ["bhk"])
                tt("dve", ra, q3[:, :, 32:64], csb, ALU.mult, ["big0", "cos_t"], ["b5"])
                tt("dve", rb, q3[:, :, 0:32], snb, ALU.mult, ["big0", "sin_t"], ["b6"])
                tt("dve", o3[:, :, 32:64], ra, rb, ALU.add, ["b5", "b6", "bhk"], ["bhk"])
                qC = qkr_[:, 0:256]; kC = qkr_[:, 256:512]; vC = projT[:, 1024:1280]
                store(oK_s[l], kC, "oK_s", r=["bhk"]); store(oV_s[l], vC, "oV_s", r=["v_hist"])
                KT = [(big[0][sl].rearrange("p (n f) -> p n f", f=256), "big0"), (big[1][sl].rearrange("p (n f) -> p n f", f=256), "big1"),
                      (xt[0][sl].rearrange("p (n f) -> p n f", f=256), "xt0"), (h1[sl].rearrange("p (n f) -> p n f", f=256), "h1")]
                cfgs = ((1920, 1), (1536, 4), (0, 16))
                pi = 0
                for ci, (r0, dd) in enumerate(cfgs):
                    for pc in range(32):
                        rows0 = r0 + dd * (pc * 4) + (dd if ci else 0) * 0
                        Kt, kk_ = KT[pi % 4]; pi += 1
                        load(Kt, ck_in[l, :, rows0:rows0 + 4 * dd:dd, :], "ck_ld" + kk_, [kk_])
                        tt("pool", Kt, Kt, bq(qC.unsqueeze(1), [N, 4, 256]), ALU.mult, [kk_, "bhk"], [kk_])
                        op("dve", "tensor_reduce", [kk_], ["band"], out=sc4[:, ci, pc * 4:(pc + 1) * 4, :], in_=Kt.rearrange("p n (h d) -> p n h d", d=64), axis=AX.X, op=ALU.add)
                pn = adab_pc[sl, 64:68]; den = rden[sl]
                tt("dve", tq, kC, qC, ALU.mult, ["bhk"], ["Kd"])
                op("dve", "tensor_reduce", ["Kd"], ["pn4"], out=pn, in_=tq.rearrange("p (h d) -> p h d", d=64), axis=AX.X, op=ALU.add)
                op("act", "activation", ["pn4"], ["pn4"], out=pn, in_=pn, func=AF.Exp, scale=HD ** -0.5)
                op("act", "activation", ["band"], ["band"], out=scores, in_=scores, func=AF.Exp, scale=HD ** -0.5)
                op("dve", "tensor_reduce", ["band"], ["rden"], out=den, in_=scores.rearrange("p (m h) -> p h m", h=4), axis=AX.X, op=ALU.add)
                op("dve", "scalar_tensor_tensor", ["pn4", "rden"], ["rden"], out=den, in0=pn, scalar=3.0, in1=den, op0=ALU.mult, op1=ALU.add)
                oC = Wv("Kh"); pcs = Wv("b7")
                tt("dve", oC.rearrange("p (h d) -> p h d", d=64), vC.rearrange("p (h d) -> p h d", d=64), bq(pn.unsqueeze(2), [N, 4, 64]), ALU.mult, ["v_hist", "pn4"], ["Kh"])
                op("dve", "tensor_scalar", ["Kh"], ["Kh"], out=oC, in0=oC, scalar1=3.0, scalar2=None, op0=ALU.mult)
                for ci, (r0, dd) in enumerate(cfgs):
                    for pc in range(32):
                        rows0 = r0 + dd * (pc * 4)
                        Vt, kk_ = KT[pi % 4]; pi += 1
                        load(Vt, cv_in[l, :, rows0:rows0 + 4 * dd:dd, :], "ck_ld" + kk_, [kk_])
                        tt("pool", Vt.rearrange("p n (h d) -> p n h d", d=64), Vt.rearrange("p n (h d) -> p n h d", d=64),
                           bq(sc4[:, ci, pc * 4:(pc + 1) * 4, :].unsqueeze(3), [N, 4, 4, 64]), ALU.mult, [kk_, "band"], [kk_])
                        op("dve", "tensor_reduce", [kk_], ["b7"], out=pcs, in_=Vt.rearrange("p n f -> p f n"), axis=AX.X, op=ALU.add)
                        tt("dve", oC, oC, pcs, ALU.add, ["Kh", "b7"], ["Kh"])
                op("dve", "reciprocal", ["rden"], ["rden"], out=den, in_=den)
                tt("dve", oC.rearrange("p (h d) -> p h d", d=64), oC.rearrange("p (h d) -> p h d", d=64), bq(den.unsqueeze(2), [N, 4, 64]), ALU.mult, ["Kh", "rden"], ["Kh"])
                tt("dve", mix[sl, 512:768], oC, sg[sl, 512:768], ALU.mult, ["Kh", "sg"], ["mixC"])
                uD = projT[:, 256:512]
                store(oD_s[l], uD, "oD_s", r=["v_hist"])
                pl = Wv("b8")
                for g, w_ in enumerate(POOL_W):
                    gs_ = slice(g * 64, (g + 1) * 64)
                    cdt = xt[0][sl, 0:(w_ - 1) * 64].rearrange("p (r c) -> p r c", c=64)
                    load(cdt, cd_in[l, :, 16 - w_:15, g * 64:(g + 1) * 64], "cd_ld", ["xt0"])
                    op("dve", "tensor_reduce", ["xt0"], ["b8"], out=pl[:, gs_], in_=cdt.rearrange("p r c -> p c r"), axis=AX.X, op=ALU.add)
                    tt("dve", pl[:, gs_], pl[:, gs_], uD[:, gs_], ALU.add, ["b8", "v_hist"], ["b8"])
                    op("dve", "scalar_tensor_tensor", ["b8", "v_hist"], ["b8"], out=pl[:, gs_], in0=pl[:, gs_], scalar=1.0 / w_, in1=uD[:, gs_], op0=ALU.mult, op1=ALU.subtract)
                for g in range(4):
                    P.op("pe", lambda e, g=g: e.transpose(out=psM[0:64, 32 + g * 16:32 + g * 16 + N], in_=pl[:, g * 64:(g + 1) * 64], identity=ident[sl, sl]), r=["b8", "ident"], w=["psM"])
                for g in range(4):
                    op("act", "copy", ["psM"], ["pTd"], out=pTd[0:64, 0:N], in_=psM[0:64, 32 + g * 16:32 + g * 16 + N])
                    P.op("pe", lambda e, g=g: e.matmul(psO[sl, g * 64:(g + 1) * 64], lhsT=pTd[:, 0:N], rhs=wpool[:, g, :], start=True, stop=True), r=["pTd", "wpool"], w=["psO"])
                tt("dve", mix[sl, 768:1024], psO[sl, 0:256], sg[sl, 768:1024], ALU.mult, ["psO", "sg"], ["mixD"])
                for kc in range(8):
                    P.op("pe", lambda e, kc=kc: e.transpose(out=psT[:, kc, 0:N], in_=mix[sl, kc * 128:(kc + 1) * 128], identity=identb[sl, sl]), r=["mixA", "mixB", "mixC", "mixD", "identb"], w=["psT"])
                op("act", "copy", ["psT"], ["hT"], out=hT[:, :, 0:N], in_=psT[:, :, 0:N])
                for nb in range(2):
                    pp, pk = next_psP()
                    for kc in range(8):
                        P.op("pe", lambda e, kc=kc, pp=pp, nb=nb: e.matmul(pp[sl, :], lhsT=hT[:, kc, 0:N], rhs=wout_b[:, kc, nb * 512:(nb + 1) * 512], start=(kc == 0), stop=(kc == 7)), r=["hT", "wout_b"], w=[pk])
                    tt("dve", h1[sl, nb * 512:(nb + 1) * 512], pp[sl, :], gate_bc[sl, nb * 512:(nb + 1) * 512], ALU.mult, [pk, "gate_bc"], ["h1"])
                tt("dve", xs_t, xs_t, h1[sl], ALU.add, ["iu0", "iu1", "h1"], ["iu0", "iu1"])
            store(y_s, xs_t, "ys_out", r=["iu0", "iu1"], w=["ys_d"])
            load(band[:], k_band, "const", ["band"])
            P.op("pool", lambda e: e.memset(v_hist[:, :, :, 64:66], 1.0), w=["v_hist"])

        if NS:
            scsT = qT[:].rearrange("p a b -> p (a b)").bitcast(F32).rearrange("p (a b) -> p a b", b=NS)[:, 0:8, :]
        for l in range(L):
            load_layer_weights(l)
            layer_setup(l)
            for i in range(NT):
                X, xk = tile_front(l, i)
                pp, pk = proj_tm(0, 512)
                P.op("act", lambda e, pp=pp, i=i: e.copy(out=iu[:, i % 2, :], in_=pp[:]), r=[pk], w=[f"iu{i % 2}"])
                for gblk in range(2):
                    pp, pk = proj_tm(1280 + gblk * 512, 512)
                    P.op("act", lambda e, pp=pp, gblk=gblk: e.activation(out=sg[:, gblk * 512:(gblk + 1) * 512], in_=pp[:], func=AF.Silu), r=[pk], w=["sg"])
                if "A" in mixers:
                    if i == 0:
                        P.op("pool", lambda e: e.memset(S_A[:], 0.0), w=["S_A"])
                    mixer_A(l, i)
                if "B" in mixers:
                    if i == 0:
                        P.op("pool", lambda e: e.memset(Z_B[:], 0.0), w=["Z_B"])
                        P.op("pool", lambda e: e.memset(XS[:, :, 0], 0.0), w=["XS"])
                    mixer_B(l, i)
                if "D" in mixers:
                    mixer_D(l, i)
                if "C" in mixers:
                    mixer_C(l, i)
                tile_back(l, i, X, xk)
            if NS:
                sample_phase(l)


        P.final_wait_all()
        P.emit()
    return nc, list(din.keys())


_CACHE = {}
MIXERS = "ABCD"


def kernel(**inputs):
    T = 4096; L = 4; NS = 16; NCORE = 8
    cfg = dict(T=T, L=L, NS=NS, mixers=MIXERS, dbg=False)
    if "nc" not in _CACHE:
        _CACHE["nc"] = build(cfg)
    nc, names = _CACHE["nc"]
    shared = {}
    shared.update(host_consts(T))
    shared.update(prep_weights(inputs, L))
    xp = np.asarray(inputs["x_prompt"], dtype=np.float32)
    cp = np.asarray(inputs["c_prompt"], dtype=np.float32)
    in_maps = []
    for c in range(NCORE):
        b = c % 4
        m = {k: shared[k] for k in names if k in shared}
        m["x_p"] = np.ascontiguousarray(xp[b])
        m["c_pT"] = np.ascontiguousarray(cp[b].reshape(8, 128).T)
        b0 = c * NS
        m["x_s"] = np.ascontiguousarray(np.asarray(inputs["x_sample"], dtype=np.float32)[b0:b0 + NS, 0])
        m["c_sT"] = np.ascontiguousarray(np.asarray(inputs["c_sample"], dtype=np.float32)[b0:b0 + NS].reshape(NS, 8, 128).transpose(2, 1, 0))
        m["sA_in"] = np.ascontiguousarray(np.asarray(inputs["state_A"])[:, b0:b0 + NS])
        m["sBw_in"] = np.ascontiguousarray(np.asarray(inputs["state_B_wkv"])[:, b0:b0 + NS])
        m["sBs_in"] = np.ascontiguousarray(np.asarray(inputs["state_B_shift"])[:, b0:b0 + NS])
        m["ck_in"] = np.ascontiguousarray(np.asarray(inputs["cache_C_k"])[:, b0:b0 + NS]).reshape(L, NS, 2048, 256)
        m["cv_in"] = np.ascontiguousarray(np.asarray(inputs["cache_C_v"])[:, b0:b0 + NS]).reshape(L, NS, 2048, 256)
        m["cd_in"] = np.ascontiguousarray(np.asarray(inputs["cache_D_pool"])[:, b0:b0 + NS])
        in_maps.append(m)
    res = run_bass_kernel_spmd(nc, in_maps, core_ids=list(range(NCORE))).results
    f32 = np.float32
    y_prompt = np.stack([res[b]["y_p"] for b in range(4)]).astype(f32)
    pA = np.stack([res[b]["oA_p"] for b in range(4)], 1).astype(f32)
    pBw = np.stack([res[b]["oBw_p"] for b in range(4)], 1).astype(f32)
    pBs = np.stack([res[b]["oBs_p"] for b in range(4)], 1).astype(f32)
    pK = np.stack([res[b]["oK_p"] for b in range(4)], 1).reshape(L, 4, 2048, 4, 64).astype(f32)
    pV = np.stack([res[b]["oV_p"] for b in range(4)], 1).reshape(L, 4, 2048, 4, 64).astype(f32)
    pD = np.stack([res[b]["oD_p"] for b in range(4)], 1).astype(f32)
    def cat(name, shape):
        if name in res[0]:
            return np.concatenate([res[c][name] for c in range(NCORE)], 1).astype(f32)
        return np.zeros(shape, f32)
    y_sample = (np.concatenate([res[c]["y_s"] for c in range(NCORE)], 0).reshape(128, 1, D).astype(f32)
                if "y_s" in res[0] else np.zeros((128, 1, D), f32))
    sA = cat("oA_s", (L, 128, 4, 64, 64)); sBw = cat("oBw_s", (L, 128, 4, 64, 64)); sBs = cat("oBs_s", (L, 128, BSW))
    sK = cat("oK_s", (L, 128, 1, 4, 64)).reshape(L, 128, 1, 4, 64); sV = cat("oV_s", (L, 128, 1, 4, 64)).reshape(L, 128, 1, 4, 64)
    sD = cat("oD_s", (L, 128, 1, 256)).reshape(L, 128, 1, 256)
    return (y_prompt, y_sample, pA, sA, pBw, sBw, pBs, sBs, pK, sK, pV, sV, pD, sD)
```
